# Optimizing a Trainium2 kernel written in Bass

```python
import math
import jax, jax.numpy as jnp
from jax import lax
import numpy as np

D_MODEL = 1024
BATCH = 8
SEQ = 2048
DEPTH = 1

HYENA_WIDTH = D_MODEL
POOL_WIDTH = D_MODEL
HYENA_ORDER = 2
SHORT_CONV = 3
FILTER_EMB = 33
FILTER_HIDDEN = 64
FAST_DECAY_PCT = 0.3
SLOW_DECAY_PCT = 1.5
DECAY_TARGET = 1e-2
MAX_DECAY = math.log(DECAY_TARGET) / FAST_DECAY_PCT
MIN_DECAY = math.log(DECAY_TARGET) / SLOW_DECAY_PCT
POOL_WINDOWS = (2, 4, 8, 16)
POOL_GROUPS = len(POOL_WINDOWS)
POOL_GROUP_WIDTH = POOL_WIDTH // POOL_GROUPS
N_BRANCHES = 2
HY_IN = (HYENA_ORDER + 1) * HYENA_WIDTH
PROJ_COLS = HY_IN + HYENA_WIDTH + 2 * POOL_WIDTH + N_BRANCHES * D_MODEL
NORM_EPS = 1e-6

kernel_name = "hyena_pool_gated_hybrid_encoder"


def rms_norm(x, g):
    xf = x.astype(jnp.float32)
    y = xf * lax.rsqrt(jnp.mean(xf * xf, axis=-1, keepdims=True) + NORM_EPS)
    return (y * g.astype(jnp.float32)).astype(x.dtype)


def centred_short_conv(u, w, b):
    L = u.shape[1]
    half = SHORT_CONV // 2
    up = jnp.pad(u, ((0, 0), (half, half), (0, 0)))
    out = up[:, 0:L] * w[0]
    for k in range(1, SHORT_CONV):
        out = out + up[:, k:k + L] * w[k]
    return out + b


def filter_features(L):
    t = jnp.linspace(0.0, 1.0, L, dtype=jnp.float32)[:, None]
    bands = (FILTER_EMB - 1) // 2
    w = 2.0 * math.pi * jnp.arange(L, dtype=jnp.float32) / L
    f = jnp.linspace(1e-4, bands - 1, bands, dtype=jnp.float32)
    ang = w[:, None] * f[None, :]
    return jnp.concatenate([t, jnp.cos(ang), -jnp.sin(ang)], axis=-1), t


def implicit_filters(L, w1, b1, w2, b2, w3, b3, w4, freq):
    z, t = filter_features(L)
    fr = freq.astype(jnp.float32)
    h = jnp.sin(fr * (z @ w1.astype(jnp.float32) + b1.astype(jnp.float32)))
    h = jnp.sin(fr * (h @ w2.astype(jnp.float32) + b2.astype(jnp.float32)))
    h = jnp.sin(fr * (h @ w3.astype(jnp.float32) + b3.astype(jnp.float32)))
    k = h @ w4.astype(jnp.float32)
    deltas = jnp.abs(jnp.linspace(MIN_DECAY, MAX_DECAY, HYENA_WIDTH, dtype=jnp.float32))
    decay = jnp.exp(-t * deltas[None, :])
    k = k.reshape(L, 2, HYENA_WIDTH) * decay[:, None, :]
    return k[:, 0], k[:, 1]


def bidir_long_conv(u, k_fwd, k_bwd, d_skip):
    B, L, C = u.shape
    n = 2 * L
    k_two = jnp.concatenate([
        k_fwd.at[0].add(k_bwd[0]),
        jnp.zeros((1, C), jnp.float32),
        k_bwd[1:][::-1],
    ], axis=0)
    uf32 = u.astype(jnp.float32)
    u_hat = jnp.fft.rfft(uf32, n=n, axis=1)
    k_hat = jnp.fft.rfft(k_two, n=n, axis=0)
    y = jnp.fft.irfft(u_hat * k_hat[None], n=n, axis=1)[:, :L]
    return (y + uf32 * d_skip.astype(jnp.float32)).astype(u.dtype)


def multiscale_pool_mixer(u, w_grp, b_grp, scale):
    B, L, C = u.shape
    uf = u.astype(jnp.float32)
    csum = jnp.concatenate([jnp.zeros((B, 1, C), jnp.float32), jnp.cumsum(uf, axis=1)], axis=1)
    pos = jnp.arange(L)
    outs = []
    for g, win in enumerate(POOL_WINDOWS):
        sl = slice(g * POOL_GROUP_WIDTH, (g + 1) * POOL_GROUP_WIDTH)
        lo = jnp.clip(pos - win // 2, 0, L)
        hi = jnp.clip(pos + (win - win // 2), 0, L)
        c = csum[:, :, sl]
        s = jnp.take(c, hi, axis=1) - jnp.take(c, lo, axis=1)
        cnt = (hi - lo).astype(jnp.float32)[None, :, None]
        outs.append(s / cnt - uf[:, :, sl])
    pooled = jnp.concatenate(outs, axis=-1).reshape(B, L, POOL_GROUPS, POOL_GROUP_WIDTH).astype(u.dtype)
    y = jnp.einsum('blgc,gcd->blgd', pooled, w_grp).reshape(B, L, C) + b_grp
    return y * scale


def setup_inputs(seed: int = 0) -> dict:
    key = jax.random.key(seed)
    ks = jax.random.split(key, 24)
    f32 = jnp.float32
    n = lambda k, shape, s: jax.random.normal(k, shape, f32) * s
    Dh, Dp, D, Hf = HYENA_WIDTH, POOL_WIDTH, D_MODEL, FILTER_HIDDEN
    return {
        "x": jax.random.normal(ks[0], (BATCH, SEQ, D), f32),
        "g_norm": 1.0 + n(ks[1], (DEPTH, D), 0.05),
        "w_in": n(ks[2], (DEPTH, D, PROJ_COLS), D ** -0.5),
        "b_in": n(ks[3], (DEPTH, PROJ_COLS), 0.02),
        "conv_w": n(ks[4], (DEPTH, SHORT_CONV, HY_IN), SHORT_CONV ** -0.5),
        "conv_b": n(ks[5], (DEPTH, HY_IN), 0.02),
        "filt_w1": n(ks[6], (DEPTH, FILTER_EMB, Hf), FILTER_EMB ** -0.5),
        "filt_b1": n(ks[7], (DEPTH, Hf), 0.1),
        "filt_w2": n(ks[8], (DEPTH, Hf, Hf), Hf ** -0.5),
        "filt_b2": n(ks[9], (DEPTH, Hf), 0.1),
        "filt_w3": n(ks[10], (DEPTH, Hf, Hf), Hf ** -0.5),
        "filt_b3": n(ks[11], (DEPTH, Hf), 0.1),
        "filt_w4": n(ks[12], (DEPTH, Hf, 2 * Dh), 0.05 * Hf ** -0.5),
        "filt_freq": 1.0 + n(ks[13], (DEPTH, Hf), 0.1),
        "hyena_d": n(ks[14], (DEPTH, Dh), 1.0),
        "w_hyena_out": n(ks[15], (DEPTH, Dh, D), Dh ** -0.5),
        "pool_w": n(ks[16], (DEPTH, POOL_GROUPS, POOL_GROUP_WIDTH, POOL_GROUP_WIDTH), POOL_GROUP_WIDTH ** -0.5),
        "pool_b": n(ks[17], (DEPTH, Dp), 0.02),
        "pool_scale": 1.0 + n(ks[18], (DEPTH, Dp), 0.1),
        "w_pool_out": n(ks[19], (DEPTH, Dp, D), Dp ** -0.5),
        "w_out": n(ks[20], (DEPTH, D, D), D ** -0.5),
        "g_final": 1.0 + n(ks[21], (D,), 0.05),
    }


def reference(x, g_norm, w_in, b_in, conv_w, conv_b, filt_w1, filt_b1, filt_w2, filt_b2,
              filt_w3, filt_b3, filt_w4, filt_freq, hyena_d, w_hyena_out, pool_w, pool_b,
              pool_scale, w_pool_out, w_out, g_final):
    B, L, D = x.shape
    Dh, Dp = HYENA_WIDTH, POOL_WIDTH
    o1 = HY_IN
    o2 = o1 + Dh
    o3 = o2 + Dp
    o4 = o3 + Dp
    for l in range(DEPTH):
        h = rms_norm(x, g_norm[l])
        proj = h @ w_in[l] + b_in[l]
        hy_in, hy_z = proj[..., :o1], proj[..., o1:o2]
        pl_in, pl_z = proj[..., o2:o3], proj[..., o3:o4]
        gates = jax.nn.sigmoid(proj[..., o4:].reshape(B, L, N_BRANCHES, D))

        uc = centred_short_conv(hy_in, conv_w[l], conv_b[l])
        x0, x1, v = uc[..., :Dh], uc[..., Dh:2 * Dh], uc[..., 2 * Dh:]
        k_fwd, k_bwd = implicit_filters(L, filt_w1[l], filt_b1[l], filt_w2[l], filt_b2[l],
                                        filt_w3[l], filt_b3[l], filt_w4[l], filt_freq[l])
        v = bidir_long_conv(v * x1, k_fwd, k_bwd, hyena_d[l])
        y_h = v * x0 * jax.nn.silu(hy_z)
        out_h = y_h @ w_hyena_out[l]

        y_p = multiscale_pool_mixer(pl_in, pool_w[l], pool_b[l], pool_scale[l]) * jax.nn.silu(pl_z)
        out_p = y_p @ w_pool_out[l]

        merged = gates[..., 0, :] * out_h + gates[..., 1, :] * out_p
        x = x + merged @ w_out[l]
    return rms_norm(x, g_final)
```

```python
import math
from contextlib import ExitStack

import numpy as np
import ml_dtypes

import concourse.bass as bass
import concourse.mybir as mybir
from concourse.bass_utils import run_bass_kernel_spmd

F32 = mybir.dt.float32
BF16 = mybir.dt.bfloat16
F32R = mybir.dt.float32r
ALU = mybir.AluOpType
AF = mybir.ActivationFunctionType

L = 2048
D = 1024
NFFT = 4096
EPS = 1e-6
ENGS = ("pe", "act", "dve", "pool", "sp")
MAGIC = 12582912.0
TWO_PI = 2.0 * math.pi
PI_SAFE = 3.141592


class Sched:
    def __init__(self, nc, n_dma_sems=28):
        self.nc = nc
        self.ops = {e: [] for e in ENGS}
        self.count = {e: 0 for e in ENGS}
        self.res = {}
        self.arena_barrier = {}
        self.waited = {e: {} for e in ENGS}
        self.n_dma = n_dma_sems
        self.dma_val = [0] * n_dma_sems
        self.dma_rr = 0
        self.dma_rr_sw = 0
        self.final_tokens = {}

    def _entry(self, key):
        e = self.res.get(key)
        if e is None:
            e = [dict(self.arena_barrier.get(key[0], {})), {}]
            self.res[key] = e
        return e

    def recycle(self, arena):
        bar = self.arena_barrier.setdefault(arena, {})
        for key in [k for k in self.res if k[0] == arena]:
            hard, rd = self.res.pop(key)
            for dct in (hard, rd):
                for pk, v in dct.items():
                    if v > bar.get(pk, 0):
                        bar[pk] = v

    def _deps(self, reads, writes):
        deps = {}
        for r in reads:
            for pk, v in self._entry(r)[0].items():
                if v > deps.get(pk, 0):
                    deps[pk] = v
        for w in writes:
            e = self._entry(w)
            for dct in e:
                for pk, v in dct.items():
                    if v > deps.get(pk, 0):
                        deps[pk] = v
        return deps

    def _commit(self, token, reads, writes):
        pk, v = token
        for r in reads:
            rd = self._entry(r)[1]
            if v > rd.get(pk, 0):
                rd[pk] = v
        for w in writes:
            e = self._entry(w)
            e[0] = {pk: v}
            e[1] = {}

    def _waits(self, eng, deps):
        out = []
        for pk, v in deps.items():
            if pk == ('E', eng) and eng in ("pe", "sp"):
                continue
            if self.waited[eng].get(pk, 0) >= v:
                continue
            self.waited[eng][pk] = v
            out.append((pk, v))
        return out

    def op(self, eng, fn, reads=(), writes=()):
        deps = self._deps(reads, writes)
        waits = self._waits(eng, deps)
        self.count[eng] += 1
        token = (('E', eng), self.count[eng])
        self.ops[eng].append(("op", waits, fn))
        self._commit(token, reads, writes)
        return token

    def dma(self, eng, pairs, reads=(), writes=(), final=False):
        if eng == "pool":
            k = self.n_dma - 8 + self.dma_rr_sw
            self.dma_rr_sw = (self.dma_rr_sw + 1) % 8
        else:
            k = self.dma_rr
            self.dma_rr = (self.dma_rr + 1) % (self.n_dma - 8)
        deps = self._deps(reads, writes)
        if self.dma_val[k] > 0:
            pk = ('D', k)
            deps[pk] = max(deps.get(pk, 0), self.dma_val[k])
        waits = self._waits(eng, deps)
        self.ops[eng].append(("dma", waits, list(pairs), k))
        self.dma_val[k] += 16 * len(pairs)
        token = (('D', k), self.dma_val[k])
        self._commit(token, reads, writes)
        if final:
            self.final_tokens[token[0]] = max(self.final_tokens.get(token[0], 0), token[1])
        return token

    def emit(self, st):
        nc = self.nc
        esem = {e: st.enter_context(nc.semaphore("es_" + e)) for e in ENGS}
        dsem = [st.enter_context(nc.semaphore("ds_%d" % i)) for i in range(self.n_dma)]
        block = st.enter_context(nc.Block())

        def semof(pk):
            return esem[pk[1]] if pk[0] == 'E' else dsem[pk[1]]

        needed = {e: set() for e in ENGS}
        for eng in ENGS:
            for item in self.ops[eng]:
                for pk, v in item[1]:
                    if pk[0] == 'E':
                        needed[pk[1]].add(v)
        rank = {e: {v: i + 1 for i, v in enumerate(sorted(needed[e]))} for e in ENGS}

        def make(eng):
            def body(e):
                idx = 0
                for item in self.ops[eng]:
                    for pk, v in item[1]:
                        e.wait_ge(semof(pk), rank[pk[1]][v] if pk[0] == 'E' else v)
                    if item[0] == "op":
                        idx += 1
                        last = item[2](e)
                        if idx in needed[eng]:
                            last.then_inc(esem[eng], 1)
                    else:
                        for (o, i) in item[2]:
                            e.dma_start(out=o, in_=i).then_inc(dsem[item[3]], 16)
                if eng == "sp":
                    for pk, v in self.final_tokens.items():
                        e.wait_ge(semof(pk), v)
            return body

        block.tensor(make("pe"))
        block.scalar(make("act"))
        block.vector(make("dve"))
        block.gpsimd(make("pool"))
        block.sync(make("sp"))


def _bf(a):
    return np.ascontiguousarray(a.astype(np.float32)).astype(ml_dtypes.bfloat16)


_CONST_CACHE = {}
_DBG_SCHED = {}


def host_constants():
    if _CONST_CACHE:
        return _CONST_CACHE
    c = {}
    flo = np.arange(1024, dtype=np.float64)
    w_lo = 2.0 * np.pi * (flo + 0.5) / NFFT
    mm = np.arange(1024, dtype=np.float64)

    def fwd2_layout(M):
        return _bf(M.reshape(8, 128, 8, 128).transpose(2, 1, 0, 3).reshape(8, 128, 1024))

    c["cef"] = fwd2_layout(np.cos(np.outer(2 * mm, w_lo)))
    c["cof"] = fwd2_layout(np.cos(np.outer(2 * mm + 1, w_lo)))
    c["sef"] = fwd2_layout(np.sin(np.outer(2 * mm, w_lo)))
    c["sof"] = fwd2_layout(np.sin(np.outer(2 * mm + 1, w_lo)))
    def inv2_layout(M):
        return _bf(M.reshape(8, 128, 4, 256).transpose(2, 1, 0, 3).reshape(4, 128, 2048))

    c["cei"] = inv2_layout(np.cos(np.outer(w_lo, 2 * mm)))
    c["sei"] = inv2_layout(np.sin(np.outer(w_lo, 2 * mm)))
    c["coi"] = inv2_layout(np.cos(np.outer(w_lo, 2 * mm + 1)))
    c["soi"] = inv2_layout(np.sin(np.outer(w_lo, 2 * mm + 1)))
    c["ident"] = _bf(np.eye(128))
    tl = np.linspace(0.0, 1.0, L, dtype=np.float32)[:, None]
    bands = 16
    wv = (2.0 * math.pi * np.arange(L, dtype=np.float32) / L).astype(np.float32)
    fv = np.linspace(1e-4, bands - 1, bands, dtype=np.float32)
    ang = (wv[:, None] * fv[None, :]).astype(np.float32)
    z = np.concatenate([tl, np.cos(ang), -np.sin(ang)], axis=-1).astype(np.float32)
    c["zT"] = np.ascontiguousarray(z.T)
    tidx = (2 * (128 * np.arange(8)[None, None, :] + np.arange(128)[:, None, None]) + np.arange(2)[None, :, None])
    c["negt"] = np.ascontiguousarray((-tl[:, 0])[tidx].reshape(128, 16)).astype(np.float32)
    max_decay = math.log(1e-2) / 0.3
    min_decay = math.log(1e-2) / 1.5
    c["deltas"] = np.abs(np.linspace(min_decay, max_decay, D, dtype=np.float32)).reshape(1, D).astype(np.float32)
    band = np.zeros((128, 20, 128), np.float32)
    pos = np.arange(L)
    for g, win in enumerate((2, 4, 8, 16)):
        lo = np.clip(pos - win // 2, 0, L)
        hi = np.clip(pos + (win - win // 2), 0, L)
        M = np.zeros((L, L), np.float32)
        for tt in range(L):
            M[lo[tt]:hi[tt], tt] = 1.0 / float(hi[tt] - lo[tt])
            M[tt, tt] -= 1.0
        band[:, 5 * g + 0, :] = M[0:128, 0:128]
        band[:, 5 * g + 1, :] = M[128:256, 128:256]
        band[:, 5 * g + 2, :] = M[L - 128:L, L - 128:L]
        band[:, 5 * g + 3, :] = M[0:128, 128:256]
        band[:, 5 * g + 4, :] = M[256:384, 128:256]
    c["band"] = _bf(band.reshape(128, 2560))
    _CONST_CACHE.update(c)
    return c


def build_program(stage=99, dbg=None):
    nc = bass.Bass("TRN2", target_bir_lowering=False)

    def din(name, shape, dt=F32):
        return nc.dram_tensor(name, list(shape), dt, kind="ExternalInput").ap()

    x_d = din("x", [L, D])
    gn_d = din("g_norm", [128, 8])
    win_d = din("w_in", [D, 8192])
    bin_d = din("b_in", [128, 64])
    cw_d = din("conv_w", [128, 72])
    cb_d = din("conv_b", [128, 24])
    fw1_d = din("filt_w1", [33, 64])
    fw2_d = din("filt_w2", [64, 64])
    fw3_d = din("filt_w3", [64, 64])
    fb_d = din("filt_b", [64, 4])
    fw4_d = din("filt_w4", [64, 2048])
    hd_d = din("hyena_d", [1, D])
    who_d = din("w_hyena_out", [D, D])
    pw_d = din("pool_w", [4, 256, 256])
    pbs_d = din("pool_bs", [128, 16])
    wpo_d = din("w_pool_out", [D, D])
    wout_d = din("w_out", [D, D])
    gf_d = din("g_final", [1, D])
    cef_d = din("cef", [8, 128, 1024], BF16)
    cof_d = din("cof", [8, 128, 1024], BF16)
    sef_d = din("sef", [8, 128, 1024], BF16)
    sof_d = din("sof", [8, 128, 1024], BF16)
    cei_d = din("cei", [4, 128, 2048], BF16)
    sei_d = din("sei", [4, 128, 2048], BF16)
    coi_d = din("coi", [4, 128, 2048], BF16)
    soi_d = din("soi", [4, 128, 2048], BF16)
    ident_d = din("ident", [128, 128], BF16)
    zT_d = din("zT", [33, L])
    negt_d = din("negt", [128, 16])
    delt_d = din("deltas", [1, D])
    band_d = din("band", [128, 2560], BF16)
    out_d = nc.dram_tensor("out", [L, D], F32, kind="ExternalOutput").ap()
    dbg_d = None
    if dbg is not None:
        dbg_d = nc.dram_tensor("dbg", [128, dbg], F32, kind="ExternalOutput").ap()

    win_v = win_d.rearrange("(j p) n -> p j n", p=128)

    with ExitStack() as st:
        def sb(name, shape, dt):
            return st.enter_context(nc.sbuf_tensor("s_" + name, list(shape), dt))

        HT = sb("HT", [128, 16384], BF16)
        KA = sb("KA", [128, 16384], BF16)
        UT = sb("UT", [128, 8192], BF16)
        YA = sb("YA", [128, 16384], BF16)
        YH = sb("YH", [128, 16384], BF16)
        DF = sb("DF", [128, 12288], BF16)
        TM = sb("TM", [128, 8192], BF16)
        gn_sb = sb("gn", [128, 8], F32)
        bin_sb = sb("bin", [128, 64], F32)
        cw_sb = sb("cw", [128, 72], F32)
        cb_sb = sb("cb", [128, 24], F32)
        fw1_sb = sb("fw1", [33, 64], F32)
        fw2_sb = sb("fw2", [64, 64], F32)
        fw3_sb = sb("fw3", [64, 64], F32)
        fb_sb = sb("fb", [64, 4], F32)
        fbf_sb = sb("fbf", [64, 4], F32)
        h3T = sb("h3T", [64, L], F32)
        w4h = sb("w4h", [64, 1024], F32)
        dlt = sb("dlt", [128, 512], F32)
        dsb = sb("dsb", [128, 512], F32)
        ddb = sb("ddb", [33, 512], BF16)
        sel = sb("sel", [33, 128], BF16)
        negt_sb = sb("negt", [128, 16], F32)
        ident = sb("ident", [128, 128], BF16)
        pbs_sb = sb("pbs", [128, 16], F32)
        ss_sb = sb("ss", [128, 16], F32)
        rs_sb = sb("rs", [128, 16], F32)
        ss2_sb = sb("ss2", [128, 16], F32)
        rs2_sb = sb("rs2", [128, 16], F32)
        ps = st.enter_context(nc.psum_tensor("ps", [128, 4096], F32))

        epsc = sb("epsc", [128, 4], F32)
        mhalf = sb("mhalf", [128, 4], F32)
        S = Sched(nc)
        _DBG_SCHED['S'] = S

        def f(e):
            e.memset(mhalf[:, :], -0.5)
            return e.memset(epsc[:, :], EPS)
        S.op("dve", f, writes=[("SM", "epsc")])

        def f(e):
            e.memset(ddb[:, :], 0.0)
            return e.memset(sel[:, :], 0.0)
        S.op("pool", f, writes=[("SM", "ddb"), ("SM", "sel")])

        def f(e):
            return e.memset(sel[0:1, :], 1.0)
        S.op("pool", f, writes=[("SM", "sel")])

        def f(e):
            return e.memset(sel[32:33, :], 1.0)
        S.op("pool", f, writes=[("SM", "sel")])

        def bank(b):
            return ps[:, b * 512:(b + 1) * 512]

        def bank_bf(b):
            return ps[:, b * 512:(b + 1) * 512].bitcast(BF16)

        def f32v(arena, off_bf, n_f32):
            return arena[:, off_bf:off_bf + 2 * n_f32].bitcast(F32)

        hT = HT[:, :].rearrange("p (j t) -> p j t", j=8)
        yh = YH[:, :].rearrange("p (j t) -> p j t", j=8)

        def mm_group(out_ap, pairs):
            def fn(e):
                n = len(pairs)
                last = None
                for i, (l, r) in enumerate(pairs):
                    last = e.matmul(out_ap, lhsT=l, rhs=r, start=(i == 0), stop=(i == n - 1))
                return last
            return fn

        def dbg_dump(ap_bf_or_f32, ncols, reads, nparts=128):
            S.dma("pool", [(dbg_d[0:nparts, 0:ncols], ap_bf_or_f32)], reads=reads, final=True)

        small = [(gn_sb[:, :], gn_d), (bin_sb[:, :], bin_d), (cw_sb[:, :], cw_d), (cb_sb[:, :], cb_d),
                 (fw1_sb[:, :], fw1_d), (fw2_sb[:, :], fw2_d), (fw3_sb[:, :], fw3_d), (fb_sb[:, :], fb_d),
                 (negt_sb[:, :], negt_d), (ident[:, :], ident_d), (pbs_sb[:, :], pbs_d)]
        S.dma("sp", small, writes=[("SM", "params")])

        def xs(a):
            ar = KA if a < 8 else YA
            return f32v(ar, (a % 8) * 2048, 1024)

        def xkey(a):
            return ("KA" if a < 8 else "YA", "x", a)

        junk = TM[:, 0:1024]

        def p1_square(a):
            def f(e):
                return e.activation(out=junk, in_=xs(a), func=AF.Square, accum_out=ss_sb[:, a:a + 1])
            S.op("act", f, reads=[xkey(a)], writes=[("SM", "ss", a), ("TM", "junk")])

        def p1_rstd(g):
            cs = slice(4 * g, 4 * g + 4)

            def f(e):
                return e.tensor_scalar(out=rs_sb[:, cs], in0=ss_sb[:, cs], scalar1=1.0 / D, scalar2=EPS,
                                       op0=ALU.mult, op1=ALU.add)
            S.op("dve", f, reads=[("SM", "ss", a) for a in range(4 * g, 4 * g + 4)], writes=[("SM", "rs", g)])

            def f(e):
                return e.tensor_tensor(out=rs_sb[:, cs], in0=rs_sb[:, cs], in1=mhalf[:, 0:4], op=ALU.pow)
            S.op("pool", f, reads=[("SM", "rs", g), ("SM", "epsc")], writes=[("SM", "rs", g)])

        def p1_norm(a):
            xn = TM[:, 2048 + (a % 2) * 1024: 2048 + (a % 2 + 1) * 1024]
            xnk = ("TM", "xn", a % 2)

            def f(e):
                return e.tensor_scalar(out=xn, in0=xs(a), scalar1=rs_sb[:, a:a + 1], scalar2=None, op0=ALU.mult)
            S.op("dve", f, reads=[xkey(a), ("SM", "rs", a // 4)], writes=[xnk])

        def p1_chunk(a):
            xn = TM[:, 2048 + (a % 2) * 1024: 2048 + (a % 2 + 1) * 1024]
            xnk = ("TM", "xn", a % 2)
            b = a % 2
            pk = ("PS", b)

            def f(e):
                last = None
                for j in range(8):
                    last = e.transpose(out=bank_bf(b)[:, j * 128:(j + 1) * 128], in_=xn[:, j * 128:(j + 1) * 128],
                                       identity=ident[:, :])
                return last
            S.op("pe", f, reads=[xnk, ("SM", "params")], writes=[pk])

            def f(e):
                return e.tensor_tensor(out=hT[:, :, a * 128:(a + 1) * 128],
                                       in0=bank_bf(b).rearrange("p (j t) -> p j t", j=8),
                                       in1=gn_sb[:, 0:8].unsqueeze(2).to_broadcast([128, 8, 128]),
                                       op=ALU.mult)
            S.op("dve", f, reads=[pk, ("SM", "params")], writes=[("HT", "c", a)])
        HTALL = [("HT", "c", a) for a in range(16)]

        zT = UT[:, 0:4096].bitcast(F32)[0:33, :]
        hA = DF[:, 0:4096].bitcast(F32)[0:64, :]
        hB = DF[:, 4096:8192].bitcast(F32)[0:64, :]
        S.dma("sp", [(zT, zT_d)], writes=[("UT", "zT")])
        for a in range(16):
            S.dma("sp", [(xs(a), x_d[a * 128:(a + 1) * 128, :])], writes=[xkey(a)])

        def f(e):
            return e.tensor_scalar(out=fbf_sb[:, 0:3], in0=fb_sb[:, 0:3], scalar1=fb_sb[:, 3:4], scalar2=None,
                                   op0=ALU.mult)
        S.op("dve", f, reads=[("SM", "params")], writes=[("SM", "fbf")])
        layers = [(fw1_sb, 33, zT, ("UT", "zT"), hA, ("DF", "hA")),
                  (fw2_sb, 64, hA, ("DF", "hA"), hB, ("DF", "hB")),
                  (fw3_sb, 64, hB, ("DF", "hB"), h3T[:, :], ("SM", "h3T"))]

        def filt_step(li, tb):
            wsb, kdim, src, srck, dst, dstk = layers[li]
            b = tb % 2
            t1 = DF[:, 8192 + b * 1024: 8192 + (b + 1) * 1024].bitcast(F32)[0:64, :]
            t2 = DF[:, 10240 + b * 1024: 10240 + (b + 1) * 1024].bitcast(F32)[0:64, :]
            t1k, t2k = ("DF", "t1", b), ("DF", "t2", b)
            pso = ps[0:64, (2 + b) * 512:(3 + b) * 512]
            S.op("pe", mm_group(pso, [(wsb[0:kdim, 0:64], src[0:kdim, tb * 512:(tb + 1) * 512])]),
                 reads=[srck, ("SM", "params")], writes=[("PS", 2 + b)])

            def f(e):
                return e.activation(out=t1, in_=pso, func=AF.Identity, bias=fbf_sb[:, li:li + 1],
                                    scale=fb_sb[:, 3:4])
            S.op("act", f, reads=[("PS", 2 + b), ("SM", "fbf"), ("SM", "params")], writes=[t1k])

            def f(e):
                return e.tensor_scalar(out=t2, in0=t1, scalar1=1.0 / TWO_PI, scalar2=MAGIC,
                                       op0=ALU.mult, op1=ALU.add)
            S.op("dve", f, reads=[t1k], writes=[t2k])

            def f(e):
                return e.tensor_scalar(out=t2, in0=t2, scalar1=-MAGIC, scalar2=-TWO_PI,
                                       op0=ALU.add, op1=ALU.mult)
            S.op("dve", f, reads=[t2k], writes=[t2k])

            def f(e):
                return e.tensor_tensor(out=t1, in0=t1, in1=t2, op=ALU.add)
            S.op("dve", f, reads=[t1k, t2k], writes=[t1k])

            def f(e):
                return e.tensor_scalar(out=t1, in0=t1, scalar1=-PI_SAFE, scalar2=PI_SAFE,
                                       op0=ALU.max, op1=ALU.min)
            S.op("dve", f, reads=[t1k], writes=[t1k])

            def f(e):
                return e.activation(out=dst[:, tb * 512:(tb + 1) * 512], in_=t1, func=AF.Sin)
            S.op("act", f, reads=[t1k], writes=[dstk])

        fsteps = [(li, tb) for li in range(3) for tb in range(4)] if stage >= 2 else []
        fi = 0

        def p1_group_chunks(g):
            nonlocal_fi = None
            base = 4 * g
            p1_norm(base)
            for a in range(base, base + 4):
                if a + 1 < base + 4:
                    p1_norm(a + 1)
                p1_chunk(a)

        for g in range(4):
            if fi < len(fsteps):
                filt_step(*fsteps[fi])
                fi += 1
            for a in range(4 * g, 4 * g + 4):
                p1_square(a)
            p1_rstd(g)
            if g >= 1:
                if fi < len(fsteps):
                    filt_step(*fsteps[fi])
                    fi += 1
                p1_group_chunks(g - 1)
                if fi < len(fsteps):
                    filt_step(*fsteps[fi])
                    fi += 1
        if fi < len(fsteps):
            filt_step(*fsteps[fi])
            fi += 1
        p1_group_chunks(3)
        while fi < len(fsteps):
            filt_step(*fsteps[fi])
            fi += 1
        if stage == 1:
            dbg_dump(HT[:, :], 16384, HTALL)
        if stage == 2:
            dbg_dump(h3T[:, :], 2048, [("SM", "h3T")], nparts=64)
        S.recycle("KA")
        S.recycle("YA")
        S.recycle("TM")
        S.recycle("DF")
        S.recycle("UT")

        SC = 2.0 / NFFT
        kE = KA[:, 0:8192].rearrange("p (a c) -> p a c", a=16)
        kO = KA[:, 8192:16384].rearrange("p (a c) -> p a c", a=16)
        uT = UT[:, :].rearrange("p (a c) -> p a c", a=16)
        Xe = YA[:, 0:4096].rearrange("p (a c) -> p a c", a=8)
        Ze = YA[:, 4096:8192].rearrange("p (a c) -> p a c", a=8)
        Xo = YA[:, 8192:12288].rearrange("p (a c) -> p a c", a=8)
        Zo = YA[:, 12288:16384].rearrange("p (a c) -> p a c", a=8)

        def tmf(i):
            return TM[:, i * 1024:(i + 1) * 1024].bitcast(F32)

        def conv_steps(pb, pbk, acc, acck, chunk, first_on_dve=False):
            if first_on_dve:
                def f(e):
                    return e.tensor_scalar(out=acc, in0=pb[:, 0:256], scalar1=cw_sb[:, chunk:chunk + 1],
                                           scalar2=cb_sb[:, chunk:chunk + 1], op0=ALU.mult, op1=ALU.add)
                S.op("dve", f, reads=[pbk, ("SM", "params")], writes=[acck])
            else:
                def f(e):
                    return e.activation(out=acc, in_=pb[:, 0:256], func=AF.Identity,
                                        bias=cb_sb[:, chunk:chunk + 1], scale=cw_sb[:, chunk:chunk + 1])
                S.op("act", f, reads=[pbk, ("SM", "params")], writes=[acck])
            for k in (1, 2):
                def f(e, k=k):
                    return e.scalar_tensor_tensor(out=acc, in0=pb[:, k:k + 256],
                                                  scalar=cw_sb[:, 24 * k + chunk:24 * k + chunk + 1],
                                                  in1=acc, op0=ALU.mult, op1=ALU.add)
                S.op("dve", f, reads=[pbk, acck, ("SM", "params")], writes=[acck])

        def proj_conv_in(psb, wslab, wkey, cc, chunk, tb, pb, pbk):
            t0 = tb * 256
            lo, hi = max(t0 - 1, 0), min(t0 + 257, L)
            W = hi - lo
            off = lo - (t0 - 1)
            pso = bank(psb)[:, 0:W]
            S.op("pe", mm_group(pso, [(wslab[:, j, cc * 128:(cc + 1) * 128], hT[:, j, lo:hi]) for j in range(8)]),
                 reads=HTALL + [wkey], writes=[("PS", psb)])

            def f(e):
                if tb == 0:
                    e.memzero(pb[:, 0:1])
                if tb == 7:
                    e.memzero(pb[:, 257:258])
                return e.activation(out=pb[:, off:off + W], in_=pso, func=AF.Identity,
                                    bias=bin_sb[:, chunk:chunk + 1], scale=1.0)
            S.op("act", f, reads=[("PS", psb), ("SM", "params")], writes=[pbk])

        who_v = who_d.rearrange("(j p) n -> p j n", p=128)
        wpo_v = wpo_d.rearrange("(j p) n -> p j n", p=128)
        wout_v = wout_d.rearrange("(j p) n -> p j n", p=128)

        def cols(v, c0):
            return v[:, :, c0:c0 + 512]
        slab_srcs = [cols(who_v, 0), cols(win_v, 6144), cols(who_v, 512), cols(win_v, 6144 + 512),
                     cols(win_v, 4096), cols(win_v, 4096 + 512),
                     cols(win_v, 5120), cols(win_v, 5120 + 512),
                     cols(wpo_v, 0), cols(win_v, 7168), cols(wpo_v, 512), cols(win_v, 7168 + 512),
                     cols(wout_v, 0), cols(wout_v, 512)]
        slab_issued = [0]
        slab_next = [0]

        def slab_slot(i):
            sl = i % 4
            if sl < 3:
                return DF[:, sl * 4096:(sl + 1) * 4096], ("DF", "ws", sl)
            return TM[:, 4096:8192], ("TM", "ws", 3)

        def slab_issue_upto(n):
            while slab_issued[0] < min(n, len(slab_srcs)):
                i = slab_issued[0]
                slab_issued[0] += 1
                flat, key = slab_slot(i)
                src = slab_srcs[i]
                if isinstance(src, str):
                    pwt_ = flat[:, 0:2048].rearrange("p (g c d) -> p g c d", g=4, c=2)
                    S.dma("pool", [(pwt_[:, g, :, :], pw_d[g].rearrange("(c p) d -> p c d", p=128)) for g in range(4)],
                          writes=[key])
                else:
                    S.dma("pool", [(flat.rearrange("p (j n) -> p j n", j=8), src)], writes=[key])

        def load_slab(_unused=None):
            i = slab_next[0]
            slab_next[0] += 1
            slab_issue_upto(i + 3)
            flat, key = slab_slot(i)
            return flat.rearrange("p (j n) -> p j n", j=8), key

        n_half = 2 if stage >= 3 else 0
        for h in range(n_half):
            S.dma("sp", [(w4h[:, 0:512], fw4_d[:, h * 512:(h + 1) * 512]),
                         (w4h[:, 512:1024], fw4_d[:, 1024 + h * 512:1024 + (h + 1) * 512]),
                         (dlt[:, :], delt_d[:, h * 512:(h + 1) * 512].partition_broadcast(128)[:, 0, :]),
                         (dsb[0:1, :], hd_d[:, h * 512:(h + 1) * 512]),
                         (dsb[32:33, :], hd_d[:, h * 512:(h + 1) * 512])],
                  writes=[("SM", "w4h"), ("SM", "dlt"), ("SM", "dsb")])

            def f(e):
                return e.tensor_copy(out=ddb[0:1, :], in_=dsb[0:1, :])
            S.op("dve", f, reads=[("SM", "dsb")], writes=[("SM", "ddb")])

            def f(e):
                return e.tensor_copy(out=ddb[32:33, :], in_=dsb[32:33, :])
            S.op("dve", f, reads=[("SM", "dsb"), ("SM", "ddb")], writes=[("SM", "ddb")])

            def f(e):
                return e.tensor_tensor(out=ddb[32:33, :], in0=dsb[32:33, :], in1=ddb[32:33, :], op=ALU.subtract)
            S.op("dve", f, reads=[("SM", "dsb"), ("SM", "ddb")], writes=[("SM", "ddb")])

            def f(e):
                return e.tensor_tensor(out=w4h[:, 0:512], in0=w4h[:, 0:512], in1=w4h[:, 512:1024], op=ALU.add)
            S.op("dve", f, reads=[("SM", "w4h")], writes=[("SM", "w4h")])

            def f(e):
                return e.scalar_tensor_tensor(out=w4h[:, 512:1024], in0=w4h[:, 512:1024], scalar=-2.0,
                                              in1=w4h[:, 0:512], op0=ALU.mult, op1=ALU.add)
            S.op("dve", f, reads=[("SM", "w4h")], writes=[("SM", "w4h")])

            wx1 = YA[:, 0:4096].rearrange("p (j n) -> p j n", j=8)
            wv = YA[:, 4096:8192].rearrange("p (j n) -> p j n", j=8)
            S.dma("pool", [(wx1, win_v[:, :, 1024 + h * 512:1024 + (h + 1) * 512])], writes=[("YA", "wx1")])
            S.dma("pool", [(wv, win_v[:, :, 2048 + h * 512:2048 + (h + 1) * 512])], writes=[("YA", "wv")])
            for tc in range(16):
                b = tc % 2
                dec = tmf(b)
                deck = ("TM", "dec", b)
                pf, pbk_ = 2 * b, 2 * b + 1
                kr, kmc = divmod(tc, 8)
                hcols = h3T[0:64, 256 * kmc + kr:256 * (kmc + 1):2]
                S.op("pe", mm_group(bank(pf), [(hcols, w4h[0:64, 0:512])]),
                     reads=[("SM", "h3T"), ("SM", "w4h")], writes=[("PS", pf)])
                S.op("pe", mm_group(bank(pbk_), [(hcols, w4h[0:64, 512:1024])]),
                     reads=[("SM", "h3T"), ("SM", "w4h")], writes=[("PS", pbk_)])

                def f(e, dec=dec, tc=tc):
                    return e.activation(out=dec, in_=dlt[:, :], func=AF.Exp, scale=negt_sb[:, tc:tc + 1])
                S.op("act", f, reads=[("SM", "dlt"), ("SM", "params")], writes=[deck])

                def f(e, dec=dec, pf=pf, tc=tc):
                    return e.tensor_tensor(out=kE[:, tc, :], in0=bank(pf), in1=dec, op=ALU.mult)
                S.op("dve", f, reads=[("PS", pf), deck], writes=[("KA", "kE", tc)])

                def f(e, dec=dec, pbk_=pbk_, tc=tc):
                    return e.tensor_tensor(out=kO[:, tc, :], in0=bank(pbk_), in1=dec, op=ALU.mult)
                S.op("dve", f, reads=[("PS", pbk_), deck], writes=[("KA", "kO", tc)])
            KALL = [("KA", "kE", tc) for tc in range(16)] + [("KA", "kO", tc) for tc in range(16)]
            if stage == 3 and h == 0:
                dbg_dump(KA[:, :], 16384, KALL)
            S.recycle("TM")
            if stage == 3:
                break

            pbx = DF[:, 0:4112].bitcast(F32)
            pbv = DF[:, 4112:8224].bitcast(F32)
            utfs = [YA[:, 8192 + p * 2048: 8192 + (p + 1) * 2048] for p in range(2)]
            ax = TM[:, 0:4096].bitcast(F32)
            av = TM[:, 4096:8192].bitcast(F32)
            kpx, kpv, kax, kav = ("DF", "pbx"), ("DF", "pbv"), ("TM", "ax"), ("TM", "av")

            def f(e):
                e.memset(pbx[:, 0:1], 0.0)
                e.memset(pbx[:, 2049:2050], 0.0)
                e.memset(pbv[:, 0:1], 0.0)
                return e.memset(pbv[:, 2049:2050], 0.0)
            S.op("dve", f, writes=[("DF", "pads")])
            ubank = [0]

            def u_proj(wslab, wkey, cc, chunk, pb, pbkey, bank0):
                for j in range(4):
                    b = bank0 + ubank[0] % 3
                    ubank[0] += 1
                    S.op("pe", mm_group(bank(b), [(wslab[:, jd, cc * 128:(cc + 1) * 128], hT[:, jd, j * 512:(j + 1) * 512])
                                                  for jd in range(8)]),
                         reads=HTALL + [wkey], writes=[("PS", b)])

                    def f(e, b=b, j=j):
                        return e.activation(out=pb[:, 1 + 512 * j:1 + 512 * (j + 1)], in_=bank(b), func=AF.Identity,
                                            bias=bin_sb[:, chunk:chunk + 1], scale=1.0)
                    S.op("act", f, reads=[("PS", b), ("SM", "params")], writes=[pbkey])

            def u_conv(pb, pbkey, acc, acck, chunk, first_on_dve):
                rd = [pbkey, ("DF", "pads"), ("SM", "params")]
                if first_on_dve:
                    def f(e):
                        return e.tensor_scalar(out=acc, in0=pb[:, 0:2048], scalar1=cw_sb[:, chunk:chunk + 1],
                                               scalar2=cb_sb[:, chunk:chunk + 1], op0=ALU.mult, op1=ALU.add)
                    S.op("dve", f, reads=rd, writes=[acck])
                else:
                    def f(e):
                        return e.activation(out=acc, in_=pb[:, 0:2048], func=AF.Identity,
                                            bias=cb_sb[:, chunk:chunk + 1], scale=cw_sb[:, chunk:chunk + 1])
                    S.op("act", f, reads=rd, writes=[acck])
                for k in (1, 2):
                    def f(e, k=k):
                        return e.scalar_tensor_tensor(out=acc, in0=pb[:, k:k + 2048],
                                                      scalar=cw_sb[:, 24 * k + chunk:24 * k + chunk + 1],
                                                      in1=acc, op0=ALU.mult, op1=ALU.add)
                    S.op("dve", f, reads=rd + [acck], writes=[acck])

            def u_stage_a(cc):
                gc = 4 * h + cc
                u_proj(wx1, ("YA", "wx1"), cc, 8 + gc, pbx, kpx, 0)
                u_conv(pbx, kpx, ax, kax, 8 + gc, False)
                u_proj(wv, ("YA", "wv"), cc, 16 + gc, pbv, kpv, 3)
                u_conv(pbv, kpv, av, kav, 16 + gc, True)

                utf, kut = utfs[cc % 2], ("YA", "ut", cc % 2)

                def f(e):
                    return e.tensor_tensor(out=utf, in0=ax, in1=av, op=ALU.mult)
                S.op("dve", f, reads=[kax, kav], writes=[kut])

            def u_stage_b(cc):
                utf, kut = utfs[cc % 2], ("YA", "ut", cc % 2)
                for q in range(4):
                    pT = 6 + q % 2

                    def f(e, q=q, pT=pT):
                        last = None
                        for k in range(4):
                            ur, umc = divmod(4 * q + k, 8)
                            last = e.transpose(out=bank_bf(pT)[:, k * 128:(k + 1) * 128],
                                               in_=utf[:, 256 * umc + ur:256 * (umc + 1):2], identity=ident[:, :])
                        return last
                    S.op("pe", f, reads=[kut, ("SM", "params")], writes=[("PS", pT)])

                    def f(e, q=q, pT=pT):
                        return e.activation(out=uT[:, 4 * q:4 * q + 4, cc * 128:(cc + 1) * 128],
                                            in_=bank_bf(pT)[:, 0:512].rearrange("p (k c) -> p k c", k=4),
                                            func=AF.Copy)
                    S.op("act", f, reads=[("PS", pT)], writes=[("UT", "uT", 4 * q + k) for k in range(4)])

            KALL = [("KA", "kE", tc) for tc in range(16)] + [("KA", "kO", tc) for tc in range(16)]
            UTALL = [("UT", "uT", a) for a in range(16)]
            mats = (cef_d, cof_d, sef_d, sof_d)
            slabs0 = []
            for m in range(4):
                ap0 = YH[:, 8192 + m * 1024: 8192 + (m + 1) * 1024]
                S.dma("sp", [(ap0, mats[m][0])], writes=[("YH", "dfs", m)])
                slabs0.append((ap0.rearrange("p (a f) -> p a f", a=8), ("YH", "dfs", m)))

            def fwd_kgroup(slabs):
                for bnk, mi, src, par in ((0, 0, kE, 0), (1, 1, kE, 1), (2, 2, kO, 0), (3, 3, kO, 1)):
                    pairs = [(slabs[mi][0][:, mc, :], src[:, 8 * par + mc, :]) for mc in range(8)]
                    if bnk == 0:
                        pairs.append((sel[0:33, :], ddb[0:33, :]))
                    S.op("pe", mm_group(bank(bnk), pairs),
                         reads=KALL + [slabs[mi][1], ("SM", "ddb"), ("SM", "sel")], writes=[("PS", bnk)])

            def fwd_agroup(slabs):
                for bnk, mi, par in ((4, 0, 0), (5, 1, 1), (6, 2, 0), (7, 3, 1)):
                    S.op("pe", mm_group(bank(bnk), [(slabs[mi][0][:, mc, :], uT[:, 8 * par + mc, :]) for mc in range(8)]),
                         reads=UTALL + [slabs[mi][1]], writes=[("PS", bnk)])

            def fwd_pq0():
                fwd_kgroup(slabs0)

            u_stage_a(0)
            for cc in range(4):
                if cc + 1 < 4:
                    u_stage_a(cc + 1)
                else:
                    fwd_pq0()
                u_stage_b(cc)
            if stage == 4 and h == 0:
                dbg_dump(UT[:, :], 8192, UTALL)
            S.recycle("TM")
            S.recycle("YA")
            S.recycle("DF")
            if stage == 4:
                break

            def ftile(i):
                if i < 8:
                    return tmf(i), ("TM", "T", i)
                return DF[:, 6144 + (i - 8) * 1024: 6144 + (i - 7) * 1024].bitcast(F32), ("DF", "T", i)

            for fc in range(8):
                slabs = []
                for m in range(4):
                    if fc == 0:
                        slabs.append(slabs0[m])
                        continue
                    sl = (4 * (fc - 1) + m) % 6
                    ap = DF[:, sl * 1024:(sl + 1) * 1024]
                    skey = ("DF", "slab", sl)
                    S.dma("sp", [(ap, mats[m][fc])], writes=[skey])
                    slabs.append((ap.rearrange("p (a f) -> p a f", a=8), skey))
                if fc > 0:
                    fwd_kgroup(slabs)
                fwd_agroup(slabs)
                if fc == 0:
                    S.recycle("YH")
                T = [ftile(i) for i in range(14)]
                PSk = [("PS", b) for b in range(8)]

                def act_copy(dst, b, scale):
                    def f(e):
                        return e.activation(out=T[dst][0], in_=bank(b), func=AF.Copy, scale=scale)
                    S.op("act", f, reads=[PSk[b]], writes=[T[dst][1]])

                def tt(eng, dst_ap, dst_key, a, b, op):
                    def res(x):
                        if x[0] == 'T':
                            return T[x[1]][0], T[x[1]][1]
                        if x[0] == 'P':
                            return bank(x[1]), PSk[x[1]]
                        return x[1], x[2]
                    (a_ap, a_k), (b_ap, b_k) = res(a), res(b)

                    def f(e):
                        return e.tensor_tensor(out=dst_ap, in0=a_ap, in1=b_ap, op=op)
                    S.op(eng, f, reads=[a_k, b_k], writes=[dst_key])

                def stt(dst, b, scalar, src, op1=ALU.add):
                    def f(e):
                        return e.scalar_tensor_tensor(out=T[dst][0], in0=bank(b), scalar=scalar, in1=T[src][0],
                                                      op0=ALU.mult, op1=op1)
                    S.op("dve", f, reads=[PSk[b], T[src][1]], writes=[T[dst][1]])
                act_copy(0, 1, SC)
                act_copy(1, 3, SC)
                stt(2, 0, SC, 0)
                stt(0, 0, SC, 0, ALU.subtract)
                stt(3, 2, SC, 1)
                stt(1, 2, -SC, 1)
                act_copy(4, 5, 1.0)
                act_copy(5, 7, 1.0)
                tt("dve", T[6][0], T[6][1], ('P', 4), ('T', 4), ALU.add)
                tt("dve", T[4][0], T[4][1], ('P', 4), ('T', 4), ALU.subtract)
                tt("dve", T[7][0], T[7][1], ('P', 6), ('T', 5), ALU.add)
                tt("dve", T[5][0], T[5][1], ('T', 5), ('P', 6), ALU.subtract)
                tt("dve", T[8][0], T[8][1], ('T', 6), ('T', 2), ALU.mult)
                tt("dve", T[9][0], T[9][1], ('T', 7), ('T', 3), ALU.mult)
                tt("dve", T[8][0], T[8][1], ('T', 8), ('T', 9), ALU.subtract)
                tt("dve", T[3][0], T[3][1], ('T', 6), ('T', 3), ALU.mult)
                tt("dve", T[2][0], T[2][1], ('T', 7), ('T', 2), ALU.mult)
                tt("dve", T[3][0], T[3][1], ('T', 3), ('T', 2), ALU.add)
                tt("dve", T[10][0], T[10][1], ('T', 4), ('T', 0), ALU.mult)
                tt("dve", T[11][0], T[11][1], ('T', 5), ('T', 1), ALU.mult)
                tt("dve", T[10][0], T[10][1], ('T', 10), ('T', 11), ALU.subtract)
                tt("dve", T[1][0], T[1][1], ('T', 4), ('T', 1), ALU.mult)
                tt("dve", T[0][0], T[0][1], ('T', 5), ('T', 0), ALU.mult)
                tt("dve", T[1][0], T[1][1], ('T', 1), ('T', 0), ALU.add)
                tt("dve", Xe[:, fc, :], ("YA", "Xe", fc), ('T', 8), ('T', 10), ALU.add)
                tt("dve", Ze[:, fc, :], ("YA", "Ze", fc), ('T', 3), ('T', 1), ALU.subtract)
                tt("dve", Xo[:, fc, :], ("YA", "Xo", fc), ('T', 8), ('T', 10), ALU.subtract)
                tt("dve", Zo[:, fc, :], ("YA", "Zo", fc), ('T', 3), ('T', 1), ALU.add)
            YALL = [("YA", n, fc) for n in ("Xe", "Ze", "Xo", "Zo") for fc in range(8)]
            if stage == 5 and h == 0:
                dbg_dump(YA[:, :], 16384, YALL)
            S.recycle("TM")
            S.recycle("KA")
            S.recycle("UT")
            S.recycle("DF")
            S.recycle("YH")
            if stage == 5:
                break

            wx0 = UT[:, 0:4096].rearrange("p (j n) -> p j n", j=8)
            wz = UT[:, 4096:8192].rearrange("p (j n) -> p j n", j=8)
            kwx0, kwz = ("UT", "wx0"), ("UT", "wz")
            S.dma("pool", [(wx0, win_v[:, :, h * 512:(h + 1) * 512])], writes=[kwx0])
            S.dma("pool", [(wz, win_v[:, :, 3072 + h * 512:3072 + (h + 1) * 512])], writes=[kwz])
            if h == 1 and stage >= 7:
                slab_issue_upto(3)
            step = 0
            sub = 0
            for tb5 in range(4):
                sl = tb5 % 2
                base = sl * 8192
                islabs = []
                for mi, src in enumerate((cei_d, sei_d, coi_d, soi_d)):
                    ap = KA[:, base + mi * 2048: base + (mi + 1) * 2048]
                    islabs.append(ap.rearrange("p (a t) -> p a t", a=8))
                S.dma("sp", [(KA[:, base + mi * 2048: base + (mi + 1) * 2048], src[tb5])
                             for mi, src in enumerate((cei_d, sei_d, coi_d, soi_d))], writes=[("KA", "inv", sl)])
                for cc in range(4):
                    gc = 4 * h + cc
                    p = step % 2
                    step += 1
                    by = p
                    cs = slice(cc * 128, (cc + 1) * 128)

                    def f(e, by=by, cs=cs, islabs=islabs):
                        last = None
                        for half_, (xa, za, ci, si) in enumerate(((Xe, Ze, 0, 1), (Xo, Zo, 2, 3))):
                            o = bank(by)[:, half_ * 256:(half_ + 1) * 256]
                            n_ = 16
                            i_ = 0
                            for fc in range(8):
                                for src, mi in ((xa, ci), (za, si)):
                                    last = e.matmul(o, lhsT=src[:, fc, cs], rhs=islabs[mi][:, fc, :],
                                                    start=(i_ == 0), stop=(i_ == n_ - 1))
                                    i_ += 1
                        return last
                    S.op("pe", f, reads=YALL + [("KA", "inv", sl)], writes=[("PS", by)])
                    for sb_ in range(2):
                        tb = 2 * tb5 + sb_
                        t0 = tb * 256
                        q = sub % 2
                        sub += 1
                        bx, bz = 2 + q, 4 + q

                        def slot(i):
                            return TM[:, i * 544:(i + 1) * 544].bitcast(F32)
                        pb0, a0, sz, gg = slot(q), slot(2 + q)[:, 0:256], slot(4 + q)[:, 0:256], slot(6 + q)[:, 0:256]
                        k0, ka0, ksz, kgg = (("TM", n, q) for n in ("pb0", "a0", "sz", "gg"))
                        proj_conv_in(bx, wx0, kwx0, cc, gc, tb, pb0, k0)
                        S.op("pe", mm_group(bank(bz)[:, 0:256],
                                            [(wz[:, j, cs], hT[:, j, t0:t0 + 256]) for j in range(8)]),
                             reads=HTALL + [kwz], writes=[("PS", bz)])

                        def f(e, sz=sz, bz=bz, gc=gc):
                            return e.activation(out=sz, in_=bank(bz)[:, 0:256], func=AF.Silu,
                                                bias=bin_sb[:, 24 + gc:25 + gc], scale=1.0)
                        S.op("act", f, reads=[("PS", bz), ("SM", "params")], writes=[ksz])
                        conv_steps(pb0, k0, a0, ka0, gc)

                        def f(e, gg=gg, a0=a0, sz=sz):
                            return e.tensor_tensor(out=gg, in0=a0, in1=sz, op=ALU.mult)
                        S.op("dve", f, reads=[ka0, ksz], writes=[kgg])

                        def f(e, by=by, gg=gg, gc=gc, t0=t0, sb_=sb_):
                            last = None
                            for r in range(2):
                                last = e.tensor_tensor(out=yh[:, gc, t0 + r:t0 + 256:2],
                                                       in0=bank(by)[:, r * 256 + sb_ * 128: r * 256 + (sb_ + 1) * 128],
                                                       in1=gg[:, r:256:2], op=ALU.mult)
                            return last
                        S.op("dve", f, reads=[("PS", by), kgg], writes=[("YH", "c", gc)])
            S.recycle("TM")
            S.recycle("KA")
            S.recycle("UT")
            S.recycle("YA")
            if h == 0:
                S.recycle("YH")
        YHALL = [("YH", "c", gc) for gc in range(8)]
        if stage == 6:
            dbg_dump(YH[:, :], 16384, YHALL)

        if stage >= 7:
            m1 = KA[:, :].rearrange("p (j t) -> p j t", j=8)
            mg = YA[:, :].rearrange("p (j t) -> p j t", j=8)

            def branch_out(wmat_v, gate_col0, src3, srckeys, combine):
                step = 0
                for half in range(2):
                    wa, wak = load_slab(wmat_v[:, :, half * 512:(half + 1) * 512])
                    wg, wgk = load_slab(win_v[:, :, gate_col0 + half * 512:gate_col0 + (half + 1) * 512])
                    for dcl in range(4):
                        dc = 4 * half + dcl
                        for tb in range(4):
                            p = step % 2
                            step += 1
                            bo, bg = 2 * p, 2 * p + 1
                            ts = slice(tb * 512, (tb + 1) * 512)
                            S.op("pe", mm_group(bank(bo), [(wa[:, j, dcl * 128:(dcl + 1) * 128], src3[:, j, ts])
                                                           for j in range(8)]),
                                 reads=srckeys + [wak], writes=[("PS", bo)])
                            S.op("pe", mm_group(bank(bg), [(wg[:, j, dcl * 128:(dcl + 1) * 128], hT[:, j, ts])
                                                           for j in range(8)]),
                                 reads=HTALL + [wgk], writes=[("PS", bg)])
                            gt = tmf(p)
                            gk = ("TM", "gt", p)
                            chunk = gate_col0 // 128 + dc

                            def f(e, gt=gt, bg=bg, chunk=chunk):
                                return e.activation(out=gt, in_=bank(bg), func=AF.Sigmoid,
                                                    bias=bin_sb[:, chunk:chunk + 1], scale=1.0)
                            S.op("act", f, reads=[("PS", bg), ("SM", "params")], writes=[gk])
                            combine(dc, ts, bo, gt, gk, p)

            def comb_h(dc, ts, bo, gt, gk, p):
                def f(e):
                    return e.tensor_tensor(out=m1[:, dc, ts], in0=bank(bo), in1=gt, op=ALU.mult)
                S.op("dve", f, reads=[("PS", bo), gk], writes=[("KA", "m1", dc)])
            branch_out(who_v, 6144, yh, YHALL, comb_h)
            M1ALL = [("KA", "m1", dc) for dc in range(8)]
            if stage == 7:
                dbg_dump(KA[:, :], 16384, M1ALL)
            S.recycle("YH")
            S.recycle("TM")

        if stage >= 8:
            plT = YH[:, :].rearrange("p (a c) -> p a c", a=16)
            for half in range(2):
                wp, wpk = load_slab(win_v[:, :, 4096 + half * 512:4096 + (half + 1) * 512])
                for tc in range(16):
                    b = tc % 2
                    S.op("pe", mm_group(bank(b), [(hT[:, j, tc * 128:(tc + 1) * 128], wp[:, j, :]) for j in range(8)]),
                         reads=HTALL + [wpk], writes=[("PS", b)])

                    def f(e, b=b, tc=tc, half=half):
                        return e.activation(out=plT[:, tc, half * 512:(half + 1) * 512], in_=bank(b), func=AF.Copy)
                    S.op("act", f, reads=[("PS", b)], writes=[("YH", "pl", tc)])
            PLALL = [("YH", "pl", tc) for tc in range(16)]
            band = UT[:, 0:2560].rearrange("p (k t) -> p k t", k=20)
            S.dma("sp", [(UT[:, 0:2560], band_d)], writes=[("UT", "band")])
            step = 0
            for gc in range(8):
                g = gc // 2
                for tq in range(4):
                    b = 2 + step % 2
                    step += 1

                    def f(e, gc=gc, g=g, tq=tq, b=b):
                        last = None
                        for a in range(4 * tq, 4 * tq + 4):
                            srcs = []
                            if a > 0:
                                srcs.append((a - 1, 3))
                            srcs.append((a, 0 if a == 0 else (2 if a == 15 else 1)))
                            if a < 15:
                                srcs.append((a + 1, 4))
                            o = bank(b)[:, (a - 4 * tq) * 128:(a - 4 * tq + 1) * 128]
                            for i, (sa, kind) in enumerate(srcs):
                                last = e.matmul(o, lhsT=plT[:, sa, gc * 128:(gc + 1) * 128], rhs=band[:, 5 * g + kind, :],
                                                start=(i == 0), stop=(i == len(srcs) - 1))
                        return last
                    S.op("pe", f, reads=PLALL + [("UT", "band")], writes=[("PS", b)])

                    def f(e, gc=gc, tq=tq, b=b):
                        return e.activation(out=mg[:, gc, tq * 512:(tq + 1) * 512], in_=bank(b), func=AF.Copy)
                    S.op("act", f, reads=[("PS", b)], writes=[("YA", "pooled", gc)])
            POALL = [("YA", "pooled", gc) for gc in range(8)]
            if stage == 8:
                dbg_dump(YA[:, :], 16384, POALL)
            S.recycle("YH")

        if stage >= 9:
            yp = YH[:, :].rearrange("p (j t) -> p j t", j=8)
            pwt = UT[:, 5120:7168].rearrange("p (g c d) -> p g c d", g=4, c=2)
            pwk = ("UT", "pw")
            S.dma("pool", [(pwt[:, g, :, :], pw_d[g].rearrange("(c p) d -> p c d", p=128)) for g in range(4)],
                  writes=[pwk])
            step = 0
            for half in range(2):
                wzp, wzk = load_slab(win_v[:, :, 5120 + half * 512:5120 + (half + 1) * 512])
                for dcl in range(4):
                    dc = 4 * half + dcl
                    g, dl = dc // 2, dc % 2
                    for tb in range(4):
                        p = step % 2
                        step += 1
                        bo, bz = 2 * p, 2 * p + 1
                        ts = slice(tb * 512, (tb + 1) * 512)
                        S.op("pe", mm_group(bank(bo), [(pwt[:, g, cl, dl * 128:(dl + 1) * 128], mg[:, 2 * g + cl, ts])
                                                       for cl in range(2)]),
                             reads=POALL + [pwk], writes=[("PS", bo)])
                        S.op("pe", mm_group(bank(bz), [(wzp[:, j, dcl * 128:(dcl + 1) * 128], hT[:, j, ts])
                                                       for j in range(8)]),
                             reads=HTALL + [wzk], writes=[("PS", bz)])
                        szp, ty = tmf(p), tmf(2 + p)
                        kszp, kty = ("TM", "szp", p), ("TM", "ty", p)

                        def f(e, szp=szp, bz=bz, dc=dc):
                            return e.activation(out=szp, in_=bank(bz), func=AF.Silu,
                                                bias=bin_sb[:, 40 + dc:41 + dc], scale=1.0)
                        S.op("act", f, reads=[("PS", bz), ("SM", "params")], writes=[kszp])

                        def f(e, ty=ty, bo=bo, dc=dc):
                            return e.tensor_scalar(out=ty, in0=bank(bo), scalar1=pbs_sb[:, dc:dc + 1],
                                                   scalar2=pbs_sb[:, 8 + dc:9 + dc], op0=ALU.add, op1=ALU.mult)
                        S.op("dve", f, reads=[("PS", bo), ("SM", "params")], writes=[kty])

                        def f(e, ty=ty, szp=szp, dc=dc, ts=ts):
                            return e.tensor_tensor(out=yp[:, dc, ts], in0=ty, in1=szp, op=ALU.mult)
                        S.op("dve", f, reads=[kty, kszp], writes=[("YH", "yp", dc)])
            YPALL = [("YH", "yp", dc) for dc in range(8)]
            if stage == 9:
                dbg_dump(YH[:, :], 16384, YPALL)
            S.recycle("YA")
            S.recycle("TM")

        if stage >= 10:
            def comb_p(dc, ts, bo, gt, gk, p):
                tp = tmf(2 + p)
                ktp = ("TM", "tp", p)

                def f(e):
                    return e.tensor_tensor(out=tp, in0=bank(bo), in1=gt, op=ALU.mult)
                S.op("dve", f, reads=[("PS", bo), gk], writes=[ktp])

                def f(e):
                    return e.tensor_tensor(out=mg[:, dc, ts], in0=tp, in1=m1[:, dc, ts], op=ALU.add)
                S.op("dve", f, reads=[ktp, ("KA", "m1", dc)], writes=[("YA", "mg", dc)])
            branch_out(wpo_v, 7168, yp, YPALL, comb_p)
            MGALL = [("YA", "mg", dc) for dc in range(8)]
            if stage == 10:
                dbg_dump(YA[:, :], 16384, MGALL)
            S.recycle("TM")
            S.recycle("UT")

            wo = []
            for half in range(2):
                wo.append(load_slab(wout_v[:, :, half * 512:(half + 1) * 512]))
            gfb = UT[:, 0:2048].bitcast(F32)
            S.dma("sp", [(gfb, gf_d.partition_broadcast(128)[:, 0, :])], writes=[("UT", "gfb")])
            S.recycle("KA")
            S.recycle("YH")
            junk5 = UT[:, 4096:5120]

            def p5_bufs(tc):
                q = tc % 4
                xt = KA[:, q * 2048:(q + 1) * 2048].bitcast(F32)
                ot = YH[:, q * 2048:(q + 1) * 2048].bitcast(F32)
                return xt, ot, ("KA", "xt", q), ("YH", "ot", q)

            def p5_xload(tc):
                xt, ot, kxt, kot = p5_bufs(tc)
                S.dma("sp", [(xt, x_d[tc * 128:(tc + 1) * 128, :])], writes=[kxt])

            for tc in range(4):
                p5_xload(tc)

            def p5_a(tc):
                p = tc % 2
                xt, ot, kxt, kot = p5_bufs(tc)
                for half in range(2):
                    b = 2 * p + half
                    S.op("pe", mm_group(bank(b), [(mg[:, j, tc * 128:(tc + 1) * 128], wo[half][0][:, j, :])
                                                  for j in range(8)]),
                         reads=MGALL + [wo[half][1]], writes=[("PS", b)])
                po = ps[:, 2 * p * 512:(2 * p + 2) * 512]

                def f(e):
                    return e.tensor_tensor(out=ot, in0=po, in1=xt, op=ALU.add)
                S.op("dve", f, reads=[("PS", 2 * p), ("PS", 2 * p + 1), kxt], writes=[kot])
                if tc + 4 < 16:
                    p5_xload(tc + 4)

                def f(e):
                    return e.activation(out=junk5, in_=ot, func=AF.Square, accum_out=ss2_sb[:, tc:tc + 1])
                S.op("act", f, reads=[kot], writes=[("UT", "junk5"), ("SM", "ss2", tc)])

                def f(e):
                    return e.activation(out=rs2_sb[:, tc:tc + 1], in_=ss2_sb[:, tc:tc + 1], func=AF.Sqrt,
                                        bias=epsc[:, 0:1], scale=1.0 / D)
                S.op("act", f, reads=[("SM", "ss2", tc), ("SM", "epsc")], writes=[("SM", "rs2", tc)])

            def p5_b(tc):
                xt, ot, kxt, kot = p5_bufs(tc)

                def f(e):
                    return e.reciprocal(out=rs2_sb[:, tc:tc + 1], in_=rs2_sb[:, tc:tc + 1])
                S.op("dve", f, reads=[("SM", "rs2", tc)], writes=[("SM", "rs2", tc)])

                def f(e):
                    return e.scalar_tensor_tensor(out=ot, in0=ot, scalar=rs2_sb[:, tc:tc + 1], in1=gfb,
                                                  op0=ALU.mult, op1=ALU.mult)
                S.op("dve", f, reads=[kot, ("SM", "rs2", tc), ("UT", "gfb")], writes=[kot])
                S.dma("sp", [(out_d[tc * 128:(tc + 1) * 128, :], ot)], reads=[kot], final=True)

            p5_a(0)
            for tc in range(16):
                if tc + 1 < 16:
                    p5_a(tc + 1)
                p5_b(tc)

        S.emit(st)
    return nc


def prep_inputs(inp):
    c = host_constants()
    f32 = lambda a: np.ascontiguousarray(np.asarray(a), dtype=np.float32)
    shared = {
        "g_norm": f32(np.asarray(inp["g_norm"])[0].reshape(8, 128).T),
        "w_in": f32(np.asarray(inp["w_in"])[0]),
        "b_in": f32(np.asarray(inp["b_in"])[0].reshape(64, 128).T),
        "conv_w": f32(np.asarray(inp["conv_w"])[0].reshape(3, 24, 128).transpose(2, 0, 1).reshape(128, 72)),
        "conv_b": f32(np.asarray(inp["conv_b"])[0].reshape(24, 128).T),
        "filt_w1": f32(np.asarray(inp["filt_w1"])[0]),
        "filt_w2": f32(np.asarray(inp["filt_w2"])[0]),
        "filt_w3": f32(np.asarray(inp["filt_w3"])[0]),
        "filt_b": f32(np.stack([np.asarray(inp["filt_b1"])[0], np.asarray(inp["filt_b2"])[0],
                                np.asarray(inp["filt_b3"])[0], np.asarray(inp["filt_freq"])[0]], axis=1)),
        "filt_w4": f32(np.asarray(inp["filt_w4"])[0]),
        "hyena_d": f32(np.asarray(inp["hyena_d"])[0].reshape(1, D)),
        "w_hyena_out": f32(np.asarray(inp["w_hyena_out"])[0]),
        "pool_w": f32(np.asarray(inp["pool_w"])[0]),
        "pool_bs": f32(np.concatenate([np.asarray(inp["pool_b"])[0].reshape(8, 128).T,
                                       np.asarray(inp["pool_scale"])[0].reshape(8, 128).T], axis=1)),
        "w_pool_out": f32(np.asarray(inp["w_pool_out"])[0]),
        "w_out": f32(np.asarray(inp["w_out"])[0]),
        "g_final": f32(np.asarray(inp["g_final"]).reshape(1, D)),
    }
    for k in ("cef", "cof", "sef", "sof", "cei", "sei", "coi", "soi", "ident", "zT", "negt", "deltas", "band"):
        shared[k] = c[k]
    x = np.asarray(inp["x"])
    in_maps = []
    for b in range(8):
        m = dict(shared)
        m["x"] = f32(x[b])
        in_maps.append(m)
    return in_maps


_PROG = {}


def kernel(**inputs):
    in_maps = prep_inputs(inputs)
    if "nc" not in _PROG:
        _PROG["nc"] = build_program()
    res = run_bass_kernel_spmd(_PROG["nc"], in_maps, core_ids=list(range(8)))
    out = np.stack([np.asarray(r["out"], dtype=np.float32) for r in res.results], axis=0)
    return out
```

```python
import math
from contextlib import ExitStack

import numpy as np
import ml_dtypes

import concourse.bass as bass
import concourse.mybir as mybir
from concourse.bass_utils import run_bass_kernel_spmd

F32 = mybir.dt.float32
BF16 = mybir.dt.bfloat16
F32R = mybir.dt.float32r
ALU = mybir.AluOpType
AF = mybir.ActivationFunctionType

L = 2048
D = 1024
NFFT = 4096
EPS = 1e-6
ENGS = ("pe", "act", "dve", "pool", "sp")
MAGIC = 12582912.0
TWO_PI = 2.0 * math.pi
PI_SAFE = 3.141592


class Sched:
    def __init__(self, nc, n_dma_sems=28):
        self.nc = nc
        self.ops = {e: [] for e in ENGS}
        self.count = {e: 0 for e in ENGS}
        self.res = {}
        self.arena_barrier = {}
        self.waited = {e: {} for e in ENGS}
        self.n_dma = n_dma_sems
        self.dma_val = [0] * n_dma_sems
        self.dma_rr = 0
        self.dma_rr_sw = 0
        self.final_tokens = {}

    def _entry(self, key):
        e = self.res.get(key)
        if e is None:
            e = [dict(self.arena_barrier.get(key[0], {})), {}]
            self.res[key] = e
        return e

    def recycle(self, arena):
        bar = self.arena_barrier.setdefault(arena, {})
        for key in [k for k in self.res if k[0] == arena]:
            hard, rd = self.res.pop(key)
            for dct in (hard, rd):
                for pk, v in dct.items():
                    if v > bar.get(pk, 0):
                        bar[pk] = v

    def _deps(self, reads, writes):
        deps = {}
        for r in reads:
            for pk, v in self._entry(r)[0].items():
                if v > deps.get(pk, 0):
                    deps[pk] = v
        for w in writes:
            e = self._entry(w)
            for dct in e:
                for pk, v in dct.items():
                    if v > deps.get(pk, 0):
                        deps[pk] = v
        return deps

    def _commit(self, token, reads, writes):
        pk, v = token
        for r in reads:
            rd = self._entry(r)[1]
            if v > rd.get(pk, 0):
                rd[pk] = v
        for w in writes:
            e = self._entry(w)
            e[0] = {pk: v}
            e[1] = {}

    def _waits(self, eng, deps):
        out = []
        for pk, v in deps.items():
            if pk == ('E', eng) and eng in ("pe", "sp"):
                continue
            if self.waited[eng].get(pk, 0) >= v:
                continue
            self.waited[eng][pk] = v
            out.append((pk, v))
        return out

    def op(self, eng, fn, reads=(), writes=()):
        deps = self._deps(reads, writes)
        waits = self._waits(eng, deps)
        self.count[eng] += 1
        token = (('E', eng), self.count[eng])
        self.ops[eng].append(("op", waits, fn))
        self._commit(token, reads, writes)
        return token

    def dma(self, eng, pairs, reads=(), writes=(), final=False):
        if eng == "pool":
            k = self.n_dma - 8 + self.dma_rr_sw
            self.dma_rr_sw = (self.dma_rr_sw + 1) % 8
        else:
            k = self.dma_rr
            self.dma_rr = (self.dma_rr + 1) % (self.n_dma - 8)
        deps = self._deps(reads, writes)
        if self.dma_val[k] > 0:
            pk = ('D', k)
            deps[pk] = max(deps.get(pk, 0), self.dma_val[k])
        waits = self._waits(eng, deps)
        self.ops[eng].append(("dma", waits, list(pairs), k))
        self.dma_val[k] += 16 * len(pairs)
        token = (('D', k), self.dma_val[k])
        self._commit(token, reads, writes)
        if final:
            self.final_tokens[token[0]] = max(self.final_tokens.get(token[0], 0), token[1])
        return token

    def emit(self, st):
        nc = self.nc
        esem = {e: st.enter_context(nc.semaphore("es_" + e)) for e in ENGS}
        dsem = [st.enter_context(nc.semaphore("ds_%d" % i)) for i in range(self.n_dma)]
        block = st.enter_context(nc.Block())

        def semof(pk):
            return esem[pk[1]] if pk[0] == 'E' else dsem[pk[1]]

        needed = {e: set() for e in ENGS}
        for eng in ENGS:
            for item in self.ops[eng]:
                for pk, v in item[1]:
                    if pk[0] == 'E':
                        needed[pk[1]].add(v)
        rank = {e: {v: i + 1 for i, v in enumerate(sorted(needed[e]))} for e in ENGS}

        def make(eng):
            def body(e):
                idx = 0
                for item in self.ops[eng]:
                    for pk, v in item[1]:
                        e.wait_ge(semof(pk), rank[pk[1]][v] if pk[0] == 'E' else v)
                    if item[0] == "op":
                        idx += 1
                        last = item[2](e)
                        if idx in needed[eng]:
                            last.then_inc(esem[eng], 1)
                    else:
                        for (o, i) in item[2]:
                            e.dma_start(out=o, in_=i).then_inc(dsem[item[3]], 16)
                if eng == "sp":
                    for pk, v in self.final_tokens.items():
                        e.wait_ge(semof(pk), v)
            return body

        block.tensor(make("pe"))
        block.scalar(make("act"))
        block.vector(make("dve"))
        block.gpsimd(make("pool"))
        block.sync(make("sp"))


def _bf(a):
    return np.ascontiguousarray(a.astype(np.float32)).astype(ml_dtypes.bfloat16)


_CONST_CACHE = {}
_DBG_SCHED = {}


def host_constants():
    if _CONST_CACHE:
        return _CONST_CACHE
    c = {}
    flo = np.arange(1024, dtype=np.float64)
    w_lo = 2.0 * np.pi * (flo + 0.5) / NFFT
    mm = np.arange(1024, dtype=np.float64)

    def fwd2_layout(M):
        return _bf(M.reshape(8, 128, 8, 128).transpose(2, 1, 0, 3).reshape(8, 128, 1024))

    c["cef"] = fwd2_layout(np.cos(np.outer(2 * mm, w_lo)))
    c["cof"] = fwd2_layout(np.cos(np.outer(2 * mm + 1, w_lo)))
    c["sef"] = fwd2_layout(np.sin(np.outer(2 * mm, w_lo)))
    c["sof"] = fwd2_layout(np.sin(np.outer(2 * mm + 1, w_lo)))
    def inv2_layout(M):
        return _bf(M.reshape(8, 128, 4, 256).transpose(2, 1, 0, 3).reshape(4, 128, 2048))

    c["cei"] = inv2_layout(np.cos(np.outer(w_lo, 2 * mm)))
    c["sei"] = inv2_layout(np.sin(np.outer(w_lo, 2 * mm)))
    c["coi"] = inv2_layout(np.cos(np.outer(w_lo, 2 * mm + 1)))
    c["soi"] = inv2_layout(np.sin(np.outer(w_lo, 2 * mm + 1)))
    c["ident"] = _bf(np.eye(128))
    tl = np.linspace(0.0, 1.0, L, dtype=np.float32)[:, None]
    bands = 16
    wv = (2.0 * math.pi * np.arange(L, dtype=np.float32) / L).astype(np.float32)
    fv = np.linspace(1e-4, bands - 1, bands, dtype=np.float32)
    ang = (wv[:, None] * fv[None, :]).astype(np.float32)
    z = np.concatenate([tl, np.cos(ang), -np.sin(ang)], axis=-1).astype(np.float32)
    c["zT"] = np.ascontiguousarray(z.T)
    tidx = (2 * (128 * np.arange(8)[None, None, :] + np.arange(128)[:, None, None]) + np.arange(2)[None, :, None])
    c["negt"] = np.ascontiguousarray((-tl[:, 0])[tidx].reshape(128, 16)).astype(np.float32)
    max_decay = math.log(1e-2) / 0.3
    min_decay = math.log(1e-2) / 1.5
    c["deltas"] = np.abs(np.linspace(min_decay, max_decay, D, dtype=np.float32)).reshape(1, D).astype(np.float32)
    band = np.zeros((128, 20, 128), np.float32)
    pos = np.arange(L)
    for g, win in enumerate((2, 4, 8, 16)):
        lo = np.clip(pos - win // 2, 0, L)
        hi = np.clip(pos + (win - win // 2), 0, L)
        M = np.zeros((L, L), np.float32)
        for tt in range(L):
            M[lo[tt]:hi[tt], tt] = 1.0 / float(hi[tt] - lo[tt])
            M[tt, tt] -= 1.0
        band[:, 5 * g + 0, :] = M[0:128, 0:128]
        band[:, 5 * g + 1, :] = M[128:256, 128:256]
        band[:, 5 * g + 2, :] = M[L - 128:L, L - 128:L]
        band[:, 5 * g + 3, :] = M[0:128, 128:256]
        band[:, 5 * g + 4, :] = M[256:384, 128:256]
    c["band"] = _bf(band.reshape(128, 2560))
    _CONST_CACHE.update(c)
    return c


def build_program(stage=99, dbg=None):
    nc = bass.Bass("TRN2", target_bir_lowering=False)

    def din(name, shape, dt=F32):
        return nc.dram_tensor(name, list(shape), dt, kind="ExternalInput").ap()

    x_d = din("x", [L, D])
    gn_d = din("g_norm", [128, 8])
    win_d = din("w_in", [D, 8192])
    bin_d = din("b_in", [128, 64])
    cw_d = din("conv_w", [128, 72])
    cb_d = din("conv_b", [128, 24])
    fw1_d = din("filt_w1", [33, 64])
    fw2_d = din("filt_w2", [64, 64])
    fw3_d = din("filt_w3", [64, 64])
    fb_d = din("filt_b", [64, 4])
    fw4_d = din("filt_w4", [64, 2048])
    hd_d = din("hyena_d", [1, D])
    who_d = din("w_hyena_out", [D, D])
    pw_d = din("pool_w", [4, 256, 256])
    pbs_d = din("pool_bs", [128, 16])
    wpo_d = din("w_pool_out", [D, D])
    wout_d = din("w_out", [D, D])
    gf_d = din("g_final", [1, D])
    cef_d = din("cef", [8, 128, 1024], BF16)
    cof_d = din("cof", [8, 128, 1024], BF16)
    sef_d = din("sef", [8, 128, 1024], BF16)
    sof_d = din("sof", [8, 128, 1024], BF16)
    cei_d = din("cei", [4, 128, 2048], BF16)
    sei_d = din("sei", [4, 128, 2048], BF16)
    coi_d = din("coi", [4, 128, 2048], BF16)
    soi_d = din("soi", [4, 128, 2048], BF16)
    ident_d = din("ident", [128, 128], BF16)
    zT_d = din("zT", [33, L])
    negt_d = din("negt", [128, 16])
    delt_d = din("deltas", [1, D])
    band_d = din("band", [128, 2560], BF16)
    out_d = nc.dram_tensor("out", [L, D], F32, kind="ExternalOutput").ap()
    dbg_d = None
    if dbg is not None:
        dbg_d = nc.dram_tensor("dbg", [128, dbg], F32, kind="ExternalOutput").ap()

    win_v = win_d.rearrange("(j p) n -> p j n", p=128)

    with ExitStack() as st:
        def sb(name, shape, dt):
            return st.enter_context(nc.sbuf_tensor("s_" + name, list(shape), dt))

        HT = sb("HT", [128, 16384], BF16)
        KA = sb("KA", [128, 16384], BF16)
        UT = sb("UT", [128, 8192], BF16)
        YA = sb("YA", [128, 16384], BF16)
        YH = sb("YH", [128, 16384], BF16)
        DF = sb("DF", [128, 12288], BF16)
        TM = sb("TM", [128, 8192], BF16)
        gn_sb = sb("gn", [128, 8], F32)
        bin_sb = sb("bin", [128, 64], F32)
        cw_sb = sb("cw", [128, 72], F32)
        cb_sb = sb("cb", [128, 24], F32)
        fw1_sb = sb("fw1", [33, 64], F32)
        fw2_sb = sb("fw2", [64, 64], F32)
        fw3_sb = sb("fw3", [64, 64], F32)
        fb_sb = sb("fb", [64, 4], F32)
        fbf_sb = sb("fbf", [64, 4], F32)
        h3T = sb("h3T", [64, L], F32)
        w4h = sb("w4h", [64, 1024], F32)
        dlt = sb("dlt", [128, 512], F32)
        dsb = sb("dsb", [128, 512], F32)
        ddb = sb("ddb", [33, 512], BF16)
        sel = sb("sel", [33, 128], BF16)
        negt_sb = sb("negt", [128, 16], F32)
        ident = sb("ident", [128, 128], BF16)
        pbs_sb = sb("pbs", [128, 16], F32)
        ss_sb = sb("ss", [128, 16], F32)
        rs_sb = sb("rs", [128, 16], F32)
        ss2_sb = sb("ss2", [128, 16], F32)
        rs2_sb = sb("rs2", [128, 16], F32)
        ps = st.enter_context(nc.psum_tensor("ps", [128, 4096], F32))

        epsc = sb("epsc", [128, 4], F32)
        mhalf = sb("mhalf", [128, 4], F32)
        S = Sched(nc)
        _DBG_SCHED['S'] = S

        def f(e):
            e.memset(mhalf[:, :], -0.5)
            return e.memset(epsc[:, :], EPS)
        S.op("dve", f, writes=[("SM", "epsc")])

        def f(e):
            e.memset(ddb[:, :], 0.0)
            return e.memset(sel[:, :], 0.0)
        S.op("pool", f, writes=[("SM", "ddb"), ("SM", "sel")])

        def f(e):
            return e.memset(sel[0:1, :], 1.0)
        S.op("pool", f, writes=[("SM", "sel")])

        def f(e):
            return e.memset(sel[32:33, :], 1.0)
        S.op("pool", f, writes=[("SM", "sel")])

        def bank(b):
            return ps[:, b * 512:(b + 1) * 512]

        def bank_bf(b):
            return ps[:, b * 512:(b + 1) * 512].bitcast(BF16)

        def f32v(arena, off_bf, n_f32):
            return arena[:, off_bf:off_bf + 2 * n_f32].bitcast(F32)

        hT = HT[:, :].rearrange("p (j t) -> p j t", j=8)
        yh = YH[:, :].rearrange("p (j t) -> p j t", j=8)

        def mm_group(out_ap, pairs):
            def fn(e):
                n = len(pairs)
                last = None
                for i, (l, r) in enumerate(pairs):
                    last = e.matmul(out_ap, lhsT=l, rhs=r, start=(i == 0), stop=(i == n - 1))
                return last
            return fn

        def dbg_dump(ap_bf_or_f32, ncols, reads, nparts=128):
            S.dma("pool", [(dbg_d[0:nparts, 0:ncols], ap_bf_or_f32)], reads=reads, final=True)

        small = [(gn_sb[:, :], gn_d), (bin_sb[:, :], bin_d), (cw_sb[:, :], cw_d), (cb_sb[:, :], cb_d),
                 (fw1_sb[:, :], fw1_d), (fw2_sb[:, :], fw2_d), (fw3_sb[:, :], fw3_d), (fb_sb[:, :], fb_d),
                 (negt_sb[:, :], negt_d), (ident[:, :], ident_d), (pbs_sb[:, :], pbs_d)]
        S.dma("sp", small, writes=[("SM", "params")])

        def xs(a):
            ar = KA if a < 8 else YA
            return f32v(ar, (a % 8) * 2048, 1024)

        def xkey(a):
            return ("KA" if a < 8 else "YA", "x", a)

        junk = TM[:, 0:1024]

        def p1_square(a):
            def f(e):
                return e.activation(out=junk, in_=xs(a), func=AF.Square, accum_out=ss_sb[:, a:a + 1])
            S.op("act", f, reads=[xkey(a)], writes=[("SM", "ss", a), ("TM", "junk")])

        def p1_rstd(g):
            cs = slice(4 * g, 4 * g + 4)

            def f(e):
                return e.tensor_scalar(out=rs_sb[:, cs], in0=ss_sb[:, cs], scalar1=1.0 / D, scalar2=EPS,
                                       op0=ALU.mult, op1=ALU.add)
            S.op("dve", f, reads=[("SM", "ss", a) for a in range(4 * g, 4 * g + 4)], writes=[("SM", "rs", g)])

            def f(e):
                return e.tensor_tensor(out=rs_sb[:, cs], in0=rs_sb[:, cs], in1=mhalf[:, 0:4], op=ALU.pow)
            S.op("pool", f, reads=[("SM", "rs", g), ("SM", "epsc")], writes=[("SM", "rs", g)])

        def p1_norm(a):
            xn = TM[:, 2048 + (a % 2) * 1024: 2048 + (a % 2 + 1) * 1024]
            xnk = ("TM", "xn", a % 2)

            def f(e):
                return e.tensor_scalar(out=xn, in0=xs(a), scalar1=rs_sb[:, a:a + 1], scalar2=None, op0=ALU.mult)
            S.op("dve", f, reads=[xkey(a), ("SM", "rs", a // 4)], writes=[xnk])

        def p1_chunk(a):
            xn = TM[:, 2048 + (a % 2) * 1024: 2048 + (a % 2 + 1) * 1024]
            xnk = ("TM", "xn", a % 2)
            b = a % 2
            pk = ("PS", b)

            def f(e):
                last = None
                for j in range(8):
                    last = e.transpose(out=bank_bf(b)[:, j * 128:(j + 1) * 128], in_=xn[:, j * 128:(j + 1) * 128],
                                       identity=ident[:, :])
                return last
            S.op("pe", f, reads=[xnk, ("SM", "params")], writes=[pk])

            def f(e):
                return e.tensor_tensor(out=hT[:, :, a * 128:(a + 1) * 128],
                                       in0=bank_bf(b).rearrange("p (j t) -> p j t", j=8),
                                       in1=gn_sb[:, 0:8].unsqueeze(2).to_broadcast([128, 8, 128]),
                                       op=ALU.mult)
            S.op("dve", f, reads=[pk, ("SM", "params")], writes=[("HT", "c", a)])
        HTALL = [("HT", "c", a) for a in range(16)]

        zT = UT[:, 0:4096].bitcast(F32)[0:33, :]
        hA = DF[:, 0:4096].bitcast(F32)[0:64, :]
        hB = DF[:, 4096:8192].bitcast(F32)[0:64, :]
        S.dma("sp", [(zT, zT_d)], writes=[("UT", "zT")])
        for a in range(16):
            S.dma("sp", [(xs(a), x_d[a * 128:(a + 1) * 128, :])], writes=[xkey(a)])

        def f(e):
            return e.tensor_scalar(out=fbf_sb[:, 0:3], in0=fb_sb[:, 0:3], scalar1=fb_sb[:, 3:4], scalar2=None,
                                   op0=ALU.mult)
        S.op("dve", f, reads=[("SM", "params")], writes=[("SM", "fbf")])
        layers = [(fw1_sb, 33, zT, ("UT", "zT"), hA, ("DF", "hA")),
                  (fw2_sb, 64, hA, ("DF", "hA"), hB, ("DF", "hB")),
                  (fw3_sb, 64, hB, ("DF", "hB"), h3T[:, :], ("SM", "h3T"))]

        def filt_step(li, tb):
            wsb, kdim, src, srck, dst, dstk = layers[li]
            b = tb % 2
            t1 = DF[:, 8192 + b * 1024: 8192 + (b + 1) * 1024].bitcast(F32)[0:64, :]
            t2 = DF[:, 10240 + b * 1024: 10240 + (b + 1) * 1024].bitcast(F32)[0:64, :]
            t1k, t2k = ("DF", "t1", b), ("DF", "t2", b)
            pso = ps[0:64, (2 + b) * 512:(3 + b) * 512]
            S.op("pe", mm_group(pso, [(wsb[0:kdim, 0:64], src[0:kdim, tb * 512:(tb + 1) * 512])]),
                 reads=[srck, ("SM", "params")], writes=[("PS", 2 + b)])

            def f(e):
                return e.activation(out=t1, in_=pso, func=AF.Identity, bias=fbf_sb[:, li:li + 1],
                                    scale=fb_sb[:, 3:4])
            S.op("act", f, reads=[("PS", 2 + b), ("SM", "fbf"), ("SM", "params")], writes=[t1k])

            def f(e):
                return e.tensor_scalar(out=t2, in0=t1, scalar1=1.0 / TWO_PI, scalar2=MAGIC,
                                       op0=ALU.mult, op1=ALU.add)
            S.op("dve", f, reads=[t1k], writes=[t2k])

            def f(e):
                return e.tensor_scalar(out=t2, in0=t2, scalar1=-MAGIC, scalar2=-TWO_PI,
                                       op0=ALU.add, op1=ALU.mult)
            S.op("dve", f, reads=[t2k], writes=[t2k])

            def f(e):
                return e.tensor_tensor(out=t1, in0=t1, in1=t2, op=ALU.add)
            S.op("dve", f, reads=[t1k, t2k], writes=[t1k])

            def f(e):
                return e.tensor_scalar(out=t1, in0=t1, scalar1=-PI_SAFE, scalar2=PI_SAFE,
                                       op0=ALU.max, op1=ALU.min)
            S.op("dve", f, reads=[t1k], writes=[t1k])

            def f(e):
                return e.activation(out=dst[:, tb * 512:(tb + 1) * 512], in_=t1, func=AF.Sin)
            S.op("act", f, reads=[t1k], writes=[dstk])

        fsteps = [(li, tb) for li in range(3) for tb in range(4)] if stage >= 2 else []
        fi = 0

        def p1_group_chunks(g):
            nonlocal_fi = None
            base = 4 * g
            p1_norm(base)
            for a in range(base, base + 4):
                if a + 1 < base + 4:
                    p1_norm(a + 1)
                p1_chunk(a)

        while fi < len(fsteps):
            filt_step(*fsteps[fi])
            fi += 1
        for g in range(4):
            for a in range(4 * g, 4 * g + 4):
                p1_square(a)
            p1_rstd(g)
            if g >= 1:
                if fi < len(fsteps):
                    filt_step(*fsteps[fi])
                    fi += 1
                p1_group_chunks(g - 1)
                if fi < len(fsteps):
                    filt_step(*fsteps[fi])
                    fi += 1
        if fi < len(fsteps):
            filt_step(*fsteps[fi])
            fi += 1
        p1_group_chunks(3)
        while fi < len(fsteps):
            filt_step(*fsteps[fi])
            fi += 1
        if stage == 1:
            dbg_dump(HT[:, :], 16384, HTALL)
        if stage == 2:
            dbg_dump(h3T[:, :], 2048, [("SM", "h3T")], nparts=64)
        S.recycle("KA")
        S.recycle("YA")
        S.recycle("TM")
        S.recycle("DF")
        S.recycle("UT")

        SC = 2.0 / NFFT
        kE = KA[:, 0:8192].rearrange("p (a c) -> p a c", a=16)
        kO = KA[:, 8192:16384].rearrange("p (a c) -> p a c", a=16)
        uT = UT[:, :].rearrange("p (a c) -> p a c", a=16)
        Xe = YA[:, 0:4096].rearrange("p (a c) -> p a c", a=8)
        Ze = YA[:, 4096:8192].rearrange("p (a c) -> p a c", a=8)
        Xo = YA[:, 8192:12288].rearrange("p (a c) -> p a c", a=8)
        Zo = YA[:, 12288:16384].rearrange("p (a c) -> p a c", a=8)

        def tmf(i):
            return TM[:, i * 1024:(i + 1) * 1024].bitcast(F32)

        def conv_steps(pb, pbk, acc, acck, chunk, first_on_dve=False):
            if first_on_dve:
                def f(e):
                    return e.tensor_scalar(out=acc, in0=pb[:, 0:256], scalar1=cw_sb[:, chunk:chunk + 1],
                                           scalar2=cb_sb[:, chunk:chunk + 1], op0=ALU.mult, op1=ALU.add)
                S.op("dve", f, reads=[pbk, ("SM", "params")], writes=[acck])
            else:
                def f(e):
                    return e.activation(out=acc, in_=pb[:, 0:256], func=AF.Identity,
                                        bias=cb_sb[:, chunk:chunk + 1], scale=cw_sb[:, chunk:chunk + 1])
                S.op("act", f, reads=[pbk, ("SM", "params")], writes=[acck])
            for k in (1, 2):
                def f(e, k=k):
                    return e.scalar_tensor_tensor(out=acc, in0=pb[:, k:k + 256],
                                                  scalar=cw_sb[:, 24 * k + chunk:24 * k + chunk + 1],
                                                  in1=acc, op0=ALU.mult, op1=ALU.add)
                S.op("dve", f, reads=[pbk, acck, ("SM", "params")], writes=[acck])

        def proj_conv_in(psb, wslab, wkey, cc, chunk, tb, pb, pbk):
            t0 = tb * 256
            lo, hi = max(t0 - 1, 0), min(t0 + 257, L)
            W = hi - lo
            off = lo - (t0 - 1)
            pso = bank(psb)[:, 0:W]
            S.op("pe", mm_group(pso, [(wslab[:, j, cc * 128:(cc + 1) * 128], hT[:, j, lo:hi]) for j in range(8)]),
                 reads=HTALL + [wkey], writes=[("PS", psb)])

            def f(e):
                if tb == 0:
                    e.memzero(pb[:, 0:1])
                if tb == 7:
                    e.memzero(pb[:, 257:258])
                return e.activation(out=pb[:, off:off + W], in_=pso, func=AF.Identity,
                                    bias=bin_sb[:, chunk:chunk + 1], scale=1.0)
            S.op("act", f, reads=[("PS", psb), ("SM", "params")], writes=[pbk])

        who_v = who_d.rearrange("(j p) n -> p j n", p=128)
        wpo_v = wpo_d.rearrange("(j p) n -> p j n", p=128)
        wout_v = wout_d.rearrange("(j p) n -> p j n", p=128)

        def cols(v, c0):
            return v[:, :, c0:c0 + 512]
        slab_srcs = [cols(who_v, 0), cols(win_v, 6144), cols(who_v, 512), cols(win_v, 6144 + 512),
                     cols(win_v, 4096), cols(win_v, 4096 + 512),
                     cols(win_v, 5120), cols(win_v, 5120 + 512),
                     cols(wpo_v, 0), cols(win_v, 7168), cols(wpo_v, 512), cols(win_v, 7168 + 512),
                     cols(wout_v, 0), cols(wout_v, 512)]
        slab_issued = [0]
        slab_next = [0]

        def slab_slot(i):
            sl = i % 4
            if sl < 3:
                return DF[:, sl * 4096:(sl + 1) * 4096], ("DF", "ws", sl)
            return TM[:, 4096:8192], ("TM", "ws", 3)

        def slab_issue_upto(n):
            while slab_issued[0] < min(n, len(slab_srcs)):
                i = slab_issued[0]
                slab_issued[0] += 1
                flat, key = slab_slot(i)
                src = slab_srcs[i]
                if isinstance(src, str):
                    pwt_ = flat[:, 0:2048].rearrange("p (g c d) -> p g c d", g=4, c=2)
                    S.dma("pool", [(pwt_[:, g, :, :], pw_d[g].rearrange("(c p) d -> p c d", p=128)) for g in range(4)],
                          writes=[key])
                else:
                    S.dma("pool", [(flat.rearrange("p (j n) -> p j n", j=8), src)], writes=[key])

        def load_slab(_unused=None):
            i = slab_next[0]
            slab_next[0] += 1
            slab_issue_upto(i + 3)
            flat, key = slab_slot(i)
            return flat.rearrange("p (j n) -> p j n", j=8), key

        n_half = 2 if stage >= 3 else 0
        for h in range(n_half):
            S.dma("sp", [(w4h[:, 0:512], fw4_d[:, h * 512:(h + 1) * 512]),
                         (w4h[:, 512:1024], fw4_d[:, 1024 + h * 512:1024 + (h + 1) * 512]),
                         (dlt[:, :], delt_d[:, h * 512:(h + 1) * 512].partition_broadcast(128)[:, 0, :]),
                         (dsb[0:1, :], hd_d[:, h * 512:(h + 1) * 512]),
                         (dsb[32:33, :], hd_d[:, h * 512:(h + 1) * 512])],
                  writes=[("SM", "w4h"), ("SM", "dlt"), ("SM", "dsb")])

            def f(e):
                return e.tensor_copy(out=ddb[0:1, :], in_=dsb[0:1, :])
            S.op("dve", f, reads=[("SM", "dsb")], writes=[("SM", "ddb")])

            def f(e):
                return e.tensor_copy(out=ddb[32:33, :], in_=dsb[32:33, :])
            S.op("dve", f, reads=[("SM", "dsb"), ("SM", "ddb")], writes=[("SM", "ddb")])

            def f(e):
                return e.tensor_tensor(out=ddb[32:33, :], in0=dsb[32:33, :], in1=ddb[32:33, :], op=ALU.subtract)
            S.op("dve", f, reads=[("SM", "dsb"), ("SM", "ddb")], writes=[("SM", "ddb")])

            def f(e):
                return e.tensor_tensor(out=w4h[:, 0:512], in0=w4h[:, 0:512], in1=w4h[:, 512:1024], op=ALU.add)
            S.op("dve", f, reads=[("SM", "w4h")], writes=[("SM", "w4h")])

            def f(e):
                return e.scalar_tensor_tensor(out=w4h[:, 512:1024], in0=w4h[:, 512:1024], scalar=-2.0,
                                              in1=w4h[:, 0:512], op0=ALU.mult, op1=ALU.add)
            S.op("dve", f, reads=[("SM", "w4h")], writes=[("SM", "w4h")])

            wx1 = YA[:, 0:4096].rearrange("p (j n) -> p j n", j=8)
            wv = YA[:, 4096:8192].rearrange("p (j n) -> p j n", j=8)
            S.dma("pool", [(wx1, win_v[:, :, 1024 + h * 512:1024 + (h + 1) * 512])], writes=[("YA", "wx1")])
            S.dma("pool", [(wv, win_v[:, :, 2048 + h * 512:2048 + (h + 1) * 512])], writes=[("YA", "wv")])
            for tc in range(16):
                b = tc % 2
                dec = tmf(b)
                deck = ("TM", "dec", b)
                pf, pbk_ = 2 * b, 2 * b + 1
                kr, kmc = divmod(tc, 8)
                hcols = h3T[0:64, 256 * kmc + kr:256 * (kmc + 1):2]
                S.op("pe", mm_group(bank(pf), [(hcols, w4h[0:64, 0:512])]),
                     reads=[("SM", "h3T"), ("SM", "w4h")], writes=[("PS", pf)])
                S.op("pe", mm_group(bank(pbk_), [(hcols, w4h[0:64, 512:1024])]),
                     reads=[("SM", "h3T"), ("SM", "w4h")], writes=[("PS", pbk_)])

                def f(e, dec=dec, tc=tc):
                    return e.activation(out=dec, in_=dlt[:, :], func=AF.Exp, scale=negt_sb[:, tc:tc + 1])
                S.op("act", f, reads=[("SM", "dlt"), ("SM", "params")], writes=[deck])

                def f(e, dec=dec, pf=pf, tc=tc):
                    return e.tensor_tensor(out=kE[:, tc, :], in0=bank(pf), in1=dec, op=ALU.mult)
                S.op("dve", f, reads=[("PS", pf), deck], writes=[("KA", "kE", tc)])

                def f(e, dec=dec, pbk_=pbk_, tc=tc):
                    return e.tensor_tensor(out=kO[:, tc, :], in0=bank(pbk_), in1=dec, op=ALU.mult)
                S.op("dve", f, reads=[("PS", pbk_), deck], writes=[("KA", "kO", tc)])
            KALL = [("KA", "kE", tc) for tc in range(16)] + [("KA", "kO", tc) for tc in range(16)]
            if stage == 3 and h == 0:
                dbg_dump(KA[:, :], 16384, KALL)
            S.recycle("TM")
            if stage == 3:
                break

            pbx = DF[:, 0:4112].bitcast(F32)
            pbv = DF[:, 4112:8224].bitcast(F32)
            utfs = [YA[:, 8192 + p * 2048: 8192 + (p + 1) * 2048] for p in range(2)]
            ax = TM[:, 0:4096].bitcast(F32)
            av = TM[:, 4096:8192].bitcast(F32)
            kpx, kpv, kax, kav = ("DF", "pbx"), ("DF", "pbv"), ("TM", "ax"), ("TM", "av")

            def f(e):
                e.memset(pbx[:, 0:1], 0.0)
                e.memset(pbx[:, 2049:2050], 0.0)
                e.memset(pbv[:, 0:1], 0.0)
                return e.memset(pbv[:, 2049:2050], 0.0)
            S.op("dve", f, writes=[("DF", "pads")])
            ubank = [0]

            def u_proj(wslab, wkey, cc, chunk, pb, pbkey, bank0):
                for j in range(4):
                    b = bank0 + ubank[0] % 3
                    ubank[0] += 1
                    S.op("pe", mm_group(bank(b), [(wslab[:, jd, cc * 128:(cc + 1) * 128], hT[:, jd, j * 512:(j + 1) * 512])
                                                  for jd in range(8)]),
                         reads=HTALL + [wkey], writes=[("PS", b)])

                    def f(e, b=b, j=j):
                        return e.activation(out=pb[:, 1 + 512 * j:1 + 512 * (j + 1)], in_=bank(b), func=AF.Identity,
                                            bias=bin_sb[:, chunk:chunk + 1], scale=1.0)
                    S.op("act", f, reads=[("PS", b), ("SM", "params")], writes=[pbkey])

            def u_conv(pb, pbkey, acc, acck, chunk, first_on_dve):
                rd = [pbkey, ("DF", "pads"), ("SM", "params")]
                if first_on_dve:
                    def f(e):
                        return e.tensor_scalar(out=acc, in0=pb[:, 0:2048], scalar1=cw_sb[:, chunk:chunk + 1],
                                               scalar2=cb_sb[:, chunk:chunk + 1], op0=ALU.mult, op1=ALU.add)
                    S.op("dve", f, reads=rd, writes=[acck])
                else:
                    def f(e):
                        return e.activation(out=acc, in_=pb[:, 0:2048], func=AF.Identity,
                                            bias=cb_sb[:, chunk:chunk + 1], scale=cw_sb[:, chunk:chunk + 1])
                    S.op("act", f, reads=rd, writes=[acck])
                for k in (1, 2):
                    def f(e, k=k):
                        return e.scalar_tensor_tensor(out=acc, in0=pb[:, k:k + 2048],
                                                      scalar=cw_sb[:, 24 * k + chunk:24 * k + chunk + 1],
                                                      in1=acc, op0=ALU.mult, op1=ALU.add)
                    S.op("dve", f, reads=rd + [acck], writes=[acck])

            def u_stage_a(cc):
                gc = 4 * h + cc
                u_proj(wx1, ("YA", "wx1"), cc, 8 + gc, pbx, kpx, 0)
                u_conv(pbx, kpx, ax, kax, 8 + gc, False)
                u_proj(wv, ("YA", "wv"), cc, 16 + gc, pbv, kpv, 3)
                u_conv(pbv, kpv, av, kav, 16 + gc, True)

                utf, kut = utfs[cc % 2], ("YA", "ut", cc % 2)

                def f(e):
                    return e.tensor_tensor(out=utf, in0=ax, in1=av, op=ALU.mult)
                S.op("dve", f, reads=[kax, kav], writes=[kut])

            def u_stage_b(cc):
                utf, kut = utfs[cc % 2], ("YA", "ut", cc % 2)
                for q in range(4):
                    pT = 6 + q % 2

                    def f(e, q=q, pT=pT):
                        last = None
                        for k in range(4):
                            ur, umc = divmod(4 * q + k, 8)
                            last = e.transpose(out=bank_bf(pT)[:, k * 128:(k + 1) * 128],
                                               in_=utf[:, 256 * umc + ur:256 * (umc + 1):2], identity=ident[:, :])
                        return last
                    S.op("pe", f, reads=[kut, ("SM", "params")], writes=[("PS", pT)])

                    def f(e, q=q, pT=pT):
                        return e.activation(out=uT[:, 4 * q:4 * q + 4, cc * 128:(cc + 1) * 128],
                                            in_=bank_bf(pT)[:, 0:512].rearrange("p (k c) -> p k c", k=4),
                                            func=AF.Copy)
                    S.op("act", f, reads=[("PS", pT)], writes=[("UT", "uT", 4 * q + k) for k in range(4)])

            KALL = [("KA", "kE", tc) for tc in range(16)] + [("KA", "kO", tc) for tc in range(16)]
            UTALL = [("UT", "uT", a) for a in range(16)]
            mats = (cef_d, cof_d, sef_d, sof_d)
            slabs0 = []
            for m in range(4):
                ap0 = YH[:, 8192 + m * 1024: 8192 + (m + 1) * 1024]
                S.dma("sp", [(ap0, mats[m][0])], writes=[("YH", "dfs", m)])
                slabs0.append((ap0.rearrange("p (a f) -> p a f", a=8), ("YH", "dfs", m)))

            def fwd_kgroup(slabs):
                for bnk, mi, src, par in ((0, 0, kE, 0), (1, 1, kE, 1), (2, 2, kO, 0), (3, 3, kO, 1)):
                    pairs = [(slabs[mi][0][:, mc, :], src[:, 8 * par + mc, :]) for mc in range(8)]
                    if bnk == 0:
                        pairs.append((sel[0:33, :], ddb[0:33, :]))
                    S.op("pe", mm_group(bank(bnk), pairs),
                         reads=KALL + [slabs[mi][1], ("SM", "ddb"), ("SM", "sel")], writes=[("PS", bnk)])

            def fwd_agroup(slabs):
                for bnk, mi, par in ((4, 0, 0), (5, 1, 1), (6, 2, 0), (7, 3, 1)):
                    S.op("pe", mm_group(bank(bnk), [(slabs[mi][0][:, mc, :], uT[:, 8 * par + mc, :]) for mc in range(8)]),
                         reads=UTALL + [slabs[mi][1]], writes=[("PS", bnk)])

            def fwd_pq0():
                fwd_kgroup(slabs0)

            u_stage_a(0)
            for cc in range(4):
                if cc + 1 < 4:
                    u_stage_a(cc + 1)
                else:
                    fwd_pq0()
                u_stage_b(cc)
            if stage == 4 and h == 0:
                dbg_dump(UT[:, :], 8192, UTALL)
            S.recycle("TM")
            S.recycle("YA")
            S.recycle("DF")
            if stage == 4:
                break

            def ftile(i):
                if i < 8:
                    return tmf(i), ("TM", "T", i)
                return DF[:, 6144 + (i - 8) * 1024: 6144 + (i - 7) * 1024].bitcast(F32), ("DF", "T", i)

            for fc in range(8):
                slabs = []
                for m in range(4):
                    if fc == 0:
                        slabs.append(slabs0[m])
                        continue
                    sl = (4 * (fc - 1) + m) % 6
                    ap = DF[:, sl * 1024:(sl + 1) * 1024]
                    skey = ("DF", "slab", sl)
                    S.dma("sp", [(ap, mats[m][fc])], writes=[skey])
                    slabs.append((ap.rearrange("p (a f) -> p a f", a=8), skey))
                if fc > 0:
                    fwd_kgroup(slabs)
                fwd_agroup(slabs)
                if fc == 0:
                    S.recycle("YH")
                T = [ftile(i) for i in range(14)]
                PSk = [("PS", b) for b in range(8)]

                def act_copy(dst, b, scale):
                    def f(e):
                        return e.activation(out=T[dst][0], in_=bank(b), func=AF.Copy, scale=scale)
                    S.op("act", f, reads=[PSk[b]], writes=[T[dst][1]])

                def tt(eng, dst_ap, dst_key, a, b, op):
                    def res(x):
                        if x[0] == 'T':
                            return T[x[1]][0], T[x[1]][1]
                        if x[0] == 'P':
                            return bank(x[1]), PSk[x[1]]
                        return x[1], x[2]
                    (a_ap, a_k), (b_ap, b_k) = res(a), res(b)

                    def f(e):
                        return e.tensor_tensor(out=dst_ap, in0=a_ap, in1=b_ap, op=op)
                    S.op(eng, f, reads=[a_k, b_k], writes=[dst_key])

                def stt(dst, b, scalar, src, op1=ALU.add):
                    def f(e):
                        return e.scalar_tensor_tensor(out=T[dst][0], in0=bank(b), scalar=scalar, in1=T[src][0],
                                                      op0=ALU.mult, op1=op1)
                    S.op("dve", f, reads=[PSk[b], T[src][1]], writes=[T[dst][1]])
                act_copy(0, 1, SC)
                act_copy(1, 3, SC)
                stt(2, 0, SC, 0)
                stt(0, 0, SC, 0, ALU.subtract)
                stt(3, 2, SC, 1)
                stt(1, 2, -SC, 1)
                act_copy(4, 5, 1.0)
                act_copy(5, 7, 1.0)
                tt("dve", T[6][0], T[6][1], ('P', 4), ('T', 4), ALU.add)
                tt("dve", T[4][0], T[4][1], ('P', 4), ('T', 4), ALU.subtract)
                tt("dve", T[7][0], T[7][1], ('P', 6), ('T', 5), ALU.add)
                tt("dve", T[5][0], T[5][1], ('T', 5), ('P', 6), ALU.subtract)
                tt("dve", T[8][0], T[8][1], ('T', 6), ('T', 2), ALU.mult)
                tt("dve", T[9][0], T[9][1], ('T', 7), ('T', 3), ALU.mult)
                tt("dve", T[8][0], T[8][1], ('T', 8), ('T', 9), ALU.subtract)
                tt("dve", T[3][0], T[3][1], ('T', 6), ('T', 3), ALU.mult)
                tt("dve", T[2][0], T[2][1], ('T', 7), ('T', 2), ALU.mult)
                tt("dve", T[3][0], T[3][1], ('T', 3), ('T', 2), ALU.add)
                tt("dve", T[10][0], T[10][1], ('T', 4), ('T', 0), ALU.mult)
                tt("dve", T[11][0], T[11][1], ('T', 5), ('T', 1), ALU.mult)
                tt("dve", T[10][0], T[10][1], ('T', 10), ('T', 11), ALU.subtract)
                tt("dve", T[1][0], T[1][1], ('T', 4), ('T', 1), ALU.mult)
                tt("dve", T[0][0], T[0][1], ('T', 5), ('T', 0), ALU.mult)
                tt("dve", T[1][0], T[1][1], ('T', 1), ('T', 0), ALU.add)
                tt("dve", Xe[:, fc, :], ("YA", "Xe", fc), ('T', 8), ('T', 10), ALU.add)
                tt("dve", Ze[:, fc, :], ("YA", "Ze", fc), ('T', 3), ('T', 1), ALU.subtract)
                tt("dve", Xo[:, fc, :], ("YA", "Xo", fc), ('T', 8), ('T', 10), ALU.subtract)
                tt("dve", Zo[:, fc, :], ("YA", "Zo", fc), ('T', 3), ('T', 1), ALU.add)
            YALL = [("YA", n, fc) for n in ("Xe", "Ze", "Xo", "Zo") for fc in range(8)]
            if stage == 5 and h == 0:
                dbg_dump(YA[:, :], 16384, YALL)
            S.recycle("TM")
            S.recycle("KA")
            S.recycle("UT")
            S.recycle("DF")
            S.recycle("YH")
            if stage == 5:
                break

            wx0 = UT[:, 0:4096].rearrange("p (j n) -> p j n", j=8)
            wz = UT[:, 4096:8192].rearrange("p (j n) -> p j n", j=8)
            kwx0, kwz = ("UT", "wx0"), ("UT", "wz")
            S.dma("pool", [(wx0, win_v[:, :, h * 512:(h + 1) * 512])], writes=[kwx0])
            S.dma("pool", [(wz, win_v[:, :, 3072 + h * 512:3072 + (h + 1) * 512])], writes=[kwz])
            if h == 1 and stage >= 7:
                slab_issue_upto(3)
            step = 0
            sub = 0
            for tb5 in range(4):
                sl = tb5 % 2
                base = sl * 8192
                islabs = []
                for mi, src in enumerate((cei_d, sei_d, coi_d, soi_d)):
                    ap = KA[:, base + mi * 2048: base + (mi + 1) * 2048]
                    islabs.append(ap.rearrange("p (a t) -> p a t", a=8))
                S.dma("sp", [(KA[:, base + mi * 2048: base + (mi + 1) * 2048], src[tb5])
                             for mi, src in enumerate((cei_d, sei_d, coi_d, soi_d))], writes=[("KA", "inv", sl)])
                for cc in range(4):
                    gc = 4 * h + cc
                    p = step % 2
                    step += 1
                    by = p
                    cs = slice(cc * 128, (cc + 1) * 128)

                    def f(e, by=by, cs=cs, islabs=islabs):
                        last = None
                        for half_, (xa, za, ci, si) in enumerate(((Xe, Ze, 0, 1), (Xo, Zo, 2, 3))):
                            o = bank(by)[:, half_ * 256:(half_ + 1) * 256]
                            n_ = 16
                            i_ = 0
                            for fc in range(8):
                                for src, mi in ((xa, ci), (za, si)):
                                    last = e.matmul(o, lhsT=src[:, fc, cs], rhs=islabs[mi][:, fc, :],
                                                    start=(i_ == 0), stop=(i_ == n_ - 1))
                                    i_ += 1
                        return last
                    S.op("pe", f, reads=YALL + [("KA", "inv", sl)], writes=[("PS", by)])
                    for sb_ in range(2):
                        tb = 2 * tb5 + sb_
                        t0 = tb * 256
                        q = sub % 2
                        sub += 1
                        bx, bz = 2 + q, 4 + q

                        def slot(i):
                            return TM[:, i * 544:(i + 1) * 544].bitcast(F32)
                        pb0, a0, sz, gg = slot(q), slot(2 + q)[:, 0:256], slot(4 + q)[:, 0:256], slot(6 + q)[:, 0:256]
                        k0, ka0, ksz, kgg = (("TM", n, q) for n in ("pb0", "a0", "sz", "gg"))
                        proj_conv_in(bx, wx0, kwx0, cc, gc, tb, pb0, k0)
                        S.op("pe", mm_group(bank(bz)[:, 0:256],
                                            [(wz[:, j, cs], hT[:, j, t0:t0 + 256]) for j in range(8)]),
                             reads=HTALL + [kwz], writes=[("PS", bz)])

                        def f(e, sz=sz, bz=bz, gc=gc):
                            return e.activation(out=sz, in_=bank(bz)[:, 0:256], func=AF.Silu,
                                                bias=bin_sb[:, 24 + gc:25 + gc], scale=1.0)
                        S.op("act", f, reads=[("PS", bz), ("SM", "params")], writes=[ksz])
                        conv_steps(pb0, k0, a0, ka0, gc)

                        def f(e, gg=gg, a0=a0, sz=sz):
                            return e.tensor_tensor(out=gg, in0=a0, in1=sz, op=ALU.mult)
                        S.op("dve", f, reads=[ka0, ksz], writes=[kgg])

                        def f(e, by=by, gg=gg, gc=gc, t0=t0, sb_=sb_):
                            last = None
                            for r in range(2):
                                last = e.tensor_tensor(out=yh[:, gc, t0 + r:t0 + 256:2],
                                                       in0=bank(by)[:, r * 256 + sb_ * 128: r * 256 + (sb_ + 1) * 128],
                                                       in1=gg[:, r:256:2], op=ALU.mult)
                            return last
                        S.op("dve", f, reads=[("PS", by), kgg], writes=[("YH", "c", gc)])
            S.recycle("TM")
            S.recycle("KA")
            S.recycle("UT")
            S.recycle("YA")
            if h == 0:
                S.recycle("YH")
        YHALL = [("YH", "c", gc) for gc in range(8)]
        if stage == 6:
            dbg_dump(YH[:, :], 16384, YHALL)

        if stage >= 7:
            m1 = KA[:, :].rearrange("p (j t) -> p j t", j=8)
            mg = YA[:, :].rearrange("p (j t) -> p j t", j=8)

            def branch_out(wmat_v, gate_col0, src3, srckeys, combine):
                step = 0
                for half in range(2):
                    wa, wak = load_slab(wmat_v[:, :, half * 512:(half + 1) * 512])
                    wg, wgk = load_slab(win_v[:, :, gate_col0 + half * 512:gate_col0 + (half + 1) * 512])
                    for dcl in range(4):
                        dc = 4 * half + dcl
                        for tb in range(4):
                            p = step % 2
                            step += 1
                            bo, bg = 2 * p, 2 * p + 1
                            ts = slice(tb * 512, (tb + 1) * 512)
                            S.op("pe", mm_group(bank(bo), [(wa[:, j, dcl * 128:(dcl + 1) * 128], src3[:, j, ts])
                                                           for j in range(8)]),
                                 reads=srckeys + [wak], writes=[("PS", bo)])
                            S.op("pe", mm_group(bank(bg), [(wg[:, j, dcl * 128:(dcl + 1) * 128], hT[:, j, ts])
                                                           for j in range(8)]),
                                 reads=HTALL + [wgk], writes=[("PS", bg)])
                            gt = tmf(p)
                            gk = ("TM", "gt", p)
                            chunk = gate_col0 // 128 + dc

                            def f(e, gt=gt, bg=bg, chunk=chunk):
                                return e.activation(out=gt, in_=bank(bg), func=AF.Sigmoid,
                                                    bias=bin_sb[:, chunk:chunk + 1], scale=1.0)
                            S.op("act", f, reads=[("PS", bg), ("SM", "params")], writes=[gk])
                            combine(dc, ts, bo, gt, gk, p)

            def comb_h(dc, ts, bo, gt, gk, p):
                def f(e):
                    return e.tensor_tensor(out=m1[:, dc, ts], in0=bank(bo), in1=gt, op=ALU.mult)
                S.op("dve", f, reads=[("PS", bo), gk], writes=[("KA", "m1", dc)])
            branch_out(who_v, 6144, yh, YHALL, comb_h)
            M1ALL = [("KA", "m1", dc) for dc in range(8)]
            if stage == 7:
                dbg_dump(KA[:, :], 16384, M1ALL)
            S.recycle("YH")
            S.recycle("TM")

        if stage >= 8:
            plT = YH[:, :].rearrange("p (a c) -> p a c", a=16)
            for half in range(2):
                wp, wpk = load_slab(win_v[:, :, 4096 + half * 512:4096 + (half + 1) * 512])
                for tc in range(16):
                    b = tc % 2
                    S.op("pe", mm_group(bank(b), [(hT[:, j, tc * 128:(tc + 1) * 128], wp[:, j, :]) for j in range(8)]),
                         reads=HTALL + [wpk], writes=[("PS", b)])

                    def f(e, b=b, tc=tc, half=half):
                        return e.activation(out=plT[:, tc, half * 512:(half + 1) * 512], in_=bank(b), func=AF.Copy)
                    S.op("act", f, reads=[("PS", b)], writes=[("YH", "pl", tc)])
            PLALL = [("YH", "pl", tc) for tc in range(16)]
            band = UT[:, 0:2560].rearrange("p (k t) -> p k t", k=20)
            S.dma("sp", [(UT[:, 0:2560], band_d)], writes=[("UT", "band")])
            step = 0
            for gc in range(8):
                g = gc // 2
                for tq in range(4):
                    b = 2 + step % 2
                    step += 1

                    def f(e, gc=gc, g=g, tq=tq, b=b):
                        last = None
                        for a in range(4 * tq, 4 * tq + 4):
                            srcs = []
                            if a > 0:
                                srcs.append((a - 1, 3))
                            srcs.append((a, 0 if a == 0 else (2 if a == 15 else 1)))
                            if a < 15:
                                srcs.append((a + 1, 4))
                            o = bank(b)[:, (a - 4 * tq) * 128:(a - 4 * tq + 1) * 128]
                            for i, (sa, kind) in enumerate(srcs):
                                last = e.matmul(o, lhsT=plT[:, sa, gc * 128:(gc + 1) * 128], rhs=band[:, 5 * g + kind, :],
                                                start=(i == 0), stop=(i == len(srcs) - 1))
                        return last
                    S.op("pe", f, reads=PLALL + [("UT", "band")], writes=[("PS", b)])

                    def f(e, gc=gc, tq=tq, b=b):
                        return e.activation(out=mg[:, gc, tq * 512:(tq + 1) * 512], in_=bank(b), func=AF.Copy)
                    S.op("act", f, reads=[("PS", b)], writes=[("YA", "pooled", gc)])
            POALL = [("YA", "pooled", gc) for gc in range(8)]
            if stage == 8:
                dbg_dump(YA[:, :], 16384, POALL)
            S.recycle("YH")

        if stage >= 9:
            yp = YH[:, :].rearrange("p (j t) -> p j t", j=8)
            pwt = UT[:, 5120:7168].rearrange("p (g c d) -> p g c d", g=4, c=2)
            pwk = ("UT", "pw")
            S.dma("pool", [(pwt[:, g, :, :], pw_d[g].rearrange("(c p) d -> p c d", p=128)) for g in range(4)],
                  writes=[pwk])
            step = 0
            for half in range(2):
                wzp, wzk = load_slab(win_v[:, :, 5120 + half * 512:5120 + (half + 1) * 512])
                for dcl in range(4):
                    dc = 4 * half + dcl
                    g, dl = dc // 2, dc % 2
                    for tb in range(4):
                        p = step % 2
                        step += 1
                        bo, bz = 2 * p, 2 * p + 1
                        ts = slice(tb * 512, (tb + 1) * 512)
                        S.op("pe", mm_group(bank(bo), [(pwt[:, g, cl, dl * 128:(dl + 1) * 128], mg[:, 2 * g + cl, ts])
                                                       for cl in range(2)]),
                             reads=POALL + [pwk], writes=[("PS", bo)])
                        S.op("pe", mm_group(bank(bz), [(wzp[:, j, dcl * 128:(dcl + 1) * 128], hT[:, j, ts])
                                                       for j in range(8)]),
                             reads=HTALL + [wzk], writes=[("PS", bz)])
                        szp, ty = tmf(p), tmf(2 + p)
                        kszp, kty = ("TM", "szp", p), ("TM", "ty", p)

                        def f(e, szp=szp, bz=bz, dc=dc):
                            return e.activation(out=szp, in_=bank(bz), func=AF.Silu,
                                                bias=bin_sb[:, 40 + dc:41 + dc], scale=1.0)
                        S.op("act", f, reads=[("PS", bz), ("SM", "params")], writes=[kszp])

                        def f(e, ty=ty, bo=bo, dc=dc):
                            return e.tensor_scalar(out=ty, in0=bank(bo), scalar1=pbs_sb[:, dc:dc + 1],
                                                   scalar2=pbs_sb[:, 8 + dc:9 + dc], op0=ALU.add, op1=ALU.mult)
                        S.op("dve", f, reads=[("PS", bo), ("SM", "params")], writes=[kty])

                        def f(e, ty=ty, szp=szp, dc=dc, ts=ts):
                            return e.tensor_tensor(out=yp[:, dc, ts], in0=ty, in1=szp, op=ALU.mult)
                        S.op("dve", f, reads=[kty, kszp], writes=[("YH", "yp", dc)])
            YPALL = [("YH", "yp", dc) for dc in range(8)]
            if stage == 9:
                dbg_dump(YH[:, :], 16384, YPALL)
            S.recycle("YA")
            S.recycle("TM")

        if stage >= 10:
            def comb_p(dc, ts, bo, gt, gk, p):
                tp = tmf(2 + p)
                ktp = ("TM", "tp", p)

                def f(e):
                    return e.tensor_tensor(out=tp, in0=bank(bo), in1=gt, op=ALU.mult)
                S.op("dve", f, reads=[("PS", bo), gk], writes=[ktp])

                def f(e):
                    return e.tensor_tensor(out=mg[:, dc, ts], in0=tp, in1=m1[:, dc, ts], op=ALU.add)
                S.op("dve", f, reads=[ktp, ("KA", "m1", dc)], writes=[("YA", "mg", dc)])
            branch_out(wpo_v, 7168, yp, YPALL, comb_p)
            MGALL = [("YA", "mg", dc) for dc in range(8)]
            if stage == 10:
                dbg_dump(YA[:, :], 16384, MGALL)
            S.recycle("TM")
            S.recycle("UT")

            wo = []
            for half in range(2):
                wo.append(load_slab(wout_v[:, :, half * 512:(half + 1) * 512]))
            gfb = UT[:, 0:2048].bitcast(F32)
            S.dma("sp", [(gfb, gf_d.partition_broadcast(128)[:, 0, :])], writes=[("UT", "gfb")])
            S.recycle("KA")
            S.recycle("YH")
            junk5 = UT[:, 4096:5120]

            def p5_bufs(tc):
                q = tc % 4
                xt = KA[:, q * 2048:(q + 1) * 2048].bitcast(F32)
                ot = YH[:, q * 2048:(q + 1) * 2048].bitcast(F32)
                return xt, ot, ("KA", "xt", q), ("YH", "ot", q)

            def p5_xload(tc):
                xt, ot, kxt, kot = p5_bufs(tc)
                S.dma("sp", [(xt, x_d[tc * 128:(tc + 1) * 128, :])], writes=[kxt])

            for tc in range(4):
                p5_xload(tc)

            def p5_a(tc):
                p = tc % 2
                xt, ot, kxt, kot = p5_bufs(tc)
                for half in range(2):
                    b = 2 * p + half
                    S.op("pe", mm_group(bank(b), [(mg[:, j, tc * 128:(tc + 1) * 128], wo[half][0][:, j, :])
                                                  for j in range(8)]),
                         reads=MGALL + [wo[half][1]], writes=[("PS", b)])
                po = ps[:, 2 * p * 512:(2 * p + 2) * 512]

                def f(e):
                    return e.tensor_tensor(out=ot, in0=po, in1=xt, op=ALU.add)
                S.op("dve", f, reads=[("PS", 2 * p), ("PS", 2 * p + 1), kxt], writes=[kot])
                if tc + 4 < 16:
                    p5_xload(tc + 4)

                def f(e):
                    return e.activation(out=junk5, in_=ot, func=AF.Square, accum_out=ss2_sb[:, tc:tc + 1])
                S.op("act", f, reads=[kot], writes=[("UT", "junk5"), ("SM", "ss2", tc)])

                def f(e):
                    return e.activation(out=rs2_sb[:, tc:tc + 1], in_=ss2_sb[:, tc:tc + 1], func=AF.Sqrt,
                                        bias=epsc[:, 0:1], scale=1.0 / D)
                S.op("act", f, reads=[("SM", "ss2", tc), ("SM", "epsc")], writes=[("SM", "rs2", tc)])

            def p5_b(tc):
                xt, ot, kxt, kot = p5_bufs(tc)

                def f(e):
                    return e.reciprocal(out=rs2_sb[:, tc:tc + 1], in_=rs2_sb[:, tc:tc + 1])
                S.op("dve", f, reads=[("SM", "rs2", tc)], writes=[("SM", "rs2", tc)])

                def f(e):
                    return e.scalar_tensor_tensor(out=ot, in0=ot, scalar=rs2_sb[:, tc:tc + 1], in1=gfb,
                                                  op0=ALU.mult, op1=ALU.mult)
                S.op("dve", f, reads=[kot, ("SM", "rs2", tc), ("UT", "gfb")], writes=[kot])
                S.dma("sp", [(out_d[tc * 128:(tc + 1) * 128, :], ot)], reads=[kot], final=True)

            p5_a(0)
            for tc in range(16):
                if tc + 1 < 16:
                    p5_a(tc + 1)
                p5_b(tc)

        S.emit(st)
    return nc


def prep_inputs(inp):
    c = host_constants()
    f32 = lambda a: np.ascontiguousarray(np.asarray(a), dtype=np.float32)
    shared = {
        "g_norm": f32(np.asarray(inp["g_norm"])[0].reshape(8, 128).T),
        "w_in": f32(np.asarray(inp["w_in"])[0]),
        "b_in": f32(np.asarray(inp["b_in"])[0].reshape(64, 128).T),
        "conv_w": f32(np.asarray(inp["conv_w"])[0].reshape(3, 24, 128).transpose(2, 0, 1).reshape(128, 72)),
        "conv_b": f32(np.asarray(inp["conv_b"])[0].reshape(24, 128).T),
        "filt_w1": f32(np.asarray(inp["filt_w1"])[0]),
        "filt_w2": f32(np.asarray(inp["filt_w2"])[0]),
        "filt_w3": f32(np.asarray(inp["filt_w3"])[0]),
        "filt_b": f32(np.stack([np.asarray(inp["filt_b1"])[0], np.asarray(inp["filt_b2"])[0],
                                np.asarray(inp["filt_b3"])[0], np.asarray(inp["filt_freq"])[0]], axis=1)),
        "filt_w4": f32(np.asarray(inp["filt_w4"])[0]),
        "hyena_d": f32(np.asarray(inp["hyena_d"])[0].reshape(1, D)),
        "w_hyena_out": f32(np.asarray(inp["w_hyena_out"])[0]),
        "pool_w": f32(np.asarray(inp["pool_w"])[0]),
        "pool_bs": f32(np.concatenate([np.asarray(inp["pool_b"])[0].reshape(8, 128).T,
                                       np.asarray(inp["pool_scale"])[0].reshape(8, 128).T], axis=1)),
        "w_pool_out": f32(np.asarray(inp["w_pool_out"])[0]),
        "w_out": f32(np.asarray(inp["w_out"])[0]),
        "g_final": f32(np.asarray(inp["g_final"]).reshape(1, D)),
    }
    for k in ("cef", "cof", "sef", "sof", "cei", "sei", "coi", "soi", "ident", "zT", "negt", "deltas", "band"):
        shared[k] = c[k]
    x = np.asarray(inp["x"])
    in_maps = []
    for b in range(8):
        m = dict(shared)
        m["x"] = f32(x[b])
        in_maps.append(m)
    return in_maps


_PROG = {}


def kernel(**inputs):
    in_maps = prep_inputs(inputs)
    if "nc" not in _PROG:
        _PROG["nc"] = build_program()
    res = run_bass_kernel_spmd(_PROG["nc"], in_maps, core_ids=list(range(8)))
    out = np.stack([np.asarray(r["out"], dtype=np.float32) for r in res.results], axis=0)
    return out
```

```python
import math
from contextlib import ExitStack

import numpy as np
import ml_dtypes

import concourse.bass as bass
import concourse.mybir as mybir
from concourse.bass_utils import run_bass_kernel_spmd

F32 = mybir.dt.float32
BF16 = mybir.dt.bfloat16
F32R = mybir.dt.float32r
ALU = mybir.AluOpType
AF = mybir.ActivationFunctionType

L = 2048
D = 1024
NFFT = 4096
EPS = 1e-6
ENGS = ("pe", "act", "dve", "pool", "sp")
MAGIC = 12582912.0
TWO_PI = 2.0 * math.pi
PI_SAFE = 3.141592


class Sched:
    def __init__(self, nc, n_dma_sems=28):
        self.nc = nc
        self.ops = {e: [] for e in ENGS}
        self.count = {e: 0 for e in ENGS}
        self.res = {}
        self.arena_barrier = {}
        self.waited = {e: {} for e in ENGS}
        self.n_dma = n_dma_sems
        self.dma_val = [0] * n_dma_sems
        self.dma_rr = 0
        self.dma_rr_sw = 0
        self.final_tokens = {}

    def _entry(self, key):
        e = self.res.get(key)
        if e is None:
            e = [dict(self.arena_barrier.get(key[0], {})), {}]
            self.res[key] = e
        return e

    def recycle(self, arena):
        bar = self.arena_barrier.setdefault(arena, {})
        for key in [k for k in self.res if k[0] == arena]:
            hard, rd = self.res.pop(key)
            for dct in (hard, rd):
                for pk, v in dct.items():
                    if v > bar.get(pk, 0):
                        bar[pk] = v

    def _deps(self, reads, writes):
        deps = {}
        for r in reads:
            for pk, v in self._entry(r)[0].items():
                if v > deps.get(pk, 0):
                    deps[pk] = v
        for w in writes:
            e = self._entry(w)
            for dct in e:
                for pk, v in dct.items():
                    if v > deps.get(pk, 0):
                        deps[pk] = v
        return deps

    def _commit(self, token, reads, writes):
        pk, v = token
        for r in reads:
            rd = self._entry(r)[1]
            if v > rd.get(pk, 0):
                rd[pk] = v
        for w in writes:
            e = self._entry(w)
            e[0] = {pk: v}
            e[1] = {}

    def _waits(self, eng, deps):
        out = []
        for pk, v in deps.items():
            if pk == ('E', eng) and eng in ("pe", "sp"):
                continue
            if self.waited[eng].get(pk, 0) >= v:
                continue
            self.waited[eng][pk] = v
            out.append((pk, v))
        return out

    def op(self, eng, fn, reads=(), writes=()):
        deps = self._deps(reads, writes)
        waits = self._waits(eng, deps)
        self.count[eng] += 1
        token = (('E', eng), self.count[eng])
        self.ops[eng].append(("op", waits, fn))
        self._commit(token, reads, writes)
        return token

    def dma(self, eng, pairs, reads=(), writes=(), final=False):
        if eng == "pool":
            k = self.n_dma - 8 + self.dma_rr_sw
            self.dma_rr_sw = (self.dma_rr_sw + 1) % 8
        else:
            k = self.dma_rr
            self.dma_rr = (self.dma_rr + 1) % (self.n_dma - 8)
        deps = self._deps(reads, writes)
        if self.dma_val[k] > 0:
            pk = ('D', k)
            deps[pk] = max(deps.get(pk, 0), self.dma_val[k])
        waits = self._waits(eng, deps)
        self.ops[eng].append(("dma", waits, list(pairs), k))
        self.dma_val[k] += 16 * len(pairs)
        token = (('D', k), self.dma_val[k])
        self._commit(token, reads, writes)
        if final:
            self.final_tokens[token[0]] = max(self.final_tokens.get(token[0], 0), token[1])
        return token

    def emit(self, st):
        nc = self.nc
        esem = {e: st.enter_context(nc.semaphore("es_" + e)) for e in ENGS}
        dsem = [st.enter_context(nc.semaphore("ds_%d" % i)) for i in range(self.n_dma)]
        block = st.enter_context(nc.Block())

        def semof(pk):
            return esem[pk[1]] if pk[0] == 'E' else dsem[pk[1]]

        needed = {e: set() for e in ENGS}
        for eng in ENGS:
            for item in self.ops[eng]:
                for pk, v in item[1]:
                    if pk[0] == 'E':
                        needed[pk[1]].add(v)
        rank = {e: {v: i + 1 for i, v in enumerate(sorted(needed[e]))} for e in ENGS}

        def make(eng):
            def body(e):
                idx = 0
                for item in self.ops[eng]:
                    for pk, v in item[1]:
                        e.wait_ge(semof(pk), rank[pk[1]][v] if pk[0] == 'E' else v)
                    if item[0] == "op":
                        idx += 1
                        last = item[2](e)
                        if idx in needed[eng]:
                            last.then_inc(esem[eng], 1)
                    else:
                        for (o, i) in item[2]:
                            e.dma_start(out=o, in_=i).then_inc(dsem[item[3]], 16)
                if eng == "sp":
                    for pk, v in self.final_tokens.items():
                        e.wait_ge(semof(pk), v)
            return body

        block.tensor(make("pe"))
        block.scalar(make("act"))
        block.vector(make("dve"))
        block.gpsimd(make("pool"))
        block.sync(make("sp"))


def _bf(a):
    return np.ascontiguousarray(a.astype(np.float32)).astype(ml_dtypes.bfloat16)


_CONST_CACHE = {}
_DBG_SCHED = {}


def host_constants():
    if _CONST_CACHE:
        return _CONST_CACHE
    c = {}
    flo = np.arange(1024, dtype=np.float64)
    w_lo = 2.0 * np.pi * (flo + 0.5) / NFFT
    mm = np.arange(1024, dtype=np.float64)

    def fwd2_layout(M):
        return _bf(M.reshape(8, 128, 8, 128).transpose(2, 1, 0, 3).reshape(8, 128, 1024))

    c["cef"] = fwd2_layout(np.cos(np.outer(2 * mm, w_lo)))
    c["cof"] = fwd2_layout(np.cos(np.outer(2 * mm + 1, w_lo)))
    c["sef"] = fwd2_layout(np.sin(np.outer(2 * mm, w_lo)))
    c["sof"] = fwd2_layout(np.sin(np.outer(2 * mm + 1, w_lo)))
    def inv2_layout(M):
        return _bf(M.reshape(8, 128, 4, 256).transpose(2, 1, 0, 3).reshape(4, 128, 2048))

    c["cei"] = inv2_layout(np.cos(np.outer(w_lo, 2 * mm)))
    c["sei"] = inv2_layout(np.sin(np.outer(w_lo, 2 * mm)))
    c["coi"] = inv2_layout(np.cos(np.outer(w_lo, 2 * mm + 1)))
    c["soi"] = inv2_layout(np.sin(np.outer(w_lo, 2 * mm + 1)))
    c["ident"] = _bf(np.eye(128))
    tl = np.linspace(0.0, 1.0, L, dtype=np.float32)[:, None]
    bands = 16
    wv = (2.0 * math.pi * np.arange(L, dtype=np.float32) / L).astype(np.float32)
    fv = np.linspace(1e-4, bands - 1, bands, dtype=np.float32)
    ang = (wv[:, None] * fv[None, :]).astype(np.float32)
    z = np.concatenate([tl, np.cos(ang), -np.sin(ang)], axis=-1).astype(np.float32)
    c["zT"] = np.ascontiguousarray(z.T)
    tidx = (2 * (128 * np.arange(8)[None, None, :] + np.arange(128)[:, None, None]) + np.arange(2)[None, :, None])
    c["negt"] = np.ascontiguousarray((-tl[:, 0])[tidx].reshape(128, 16)).astype(np.float32)
    max_decay = math.log(1e-2) / 0.3
    min_decay = math.log(1e-2) / 1.5
    c["deltas"] = np.abs(np.linspace(min_decay, max_decay, D, dtype=np.float32)).reshape(1, D).astype(np.float32)
    band = np.zeros((128, 20, 128), np.float32)
    pos = np.arange(L)
    for g, win in enumerate((2, 4, 8, 16)):
        lo = np.clip(pos - win // 2, 0, L)
        hi = np.clip(pos + (win - win // 2), 0, L)
        M = np.zeros((L, L), np.float32)
        for tt in range(L):
            M[lo[tt]:hi[tt], tt] = 1.0 / float(hi[tt] - lo[tt])
            M[tt, tt] -= 1.0
        band[:, 5 * g + 0, :] = M[0:128, 0:128]
        band[:, 5 * g + 1, :] = M[128:256, 128:256]
        band[:, 5 * g + 2, :] = M[L - 128:L, L - 128:L]
        band[:, 5 * g + 3, :] = M[0:128, 128:256]
        band[:, 5 * g + 4, :] = M[256:384, 128:256]
    c["band"] = _bf(band.reshape(128, 2560))
    _CONST_CACHE.update(c)
    return c


def build_program(stage=99, dbg=None):
    nc = bass.Bass("TRN2", target_bir_lowering=False)

    def din(name, shape, dt=F32):
        return nc.dram_tensor(name, list(shape), dt, kind="ExternalInput").ap()

    x_d = din("x", [L, D])
    gn_d = din("g_norm", [128, 8])
    win_d = din("w_in", [D, 8192])
    bin_d = din("b_in", [128, 64])
    cw_d = din("conv_w", [128, 72])
    cb_d = din("conv_b", [128, 24])
    fw1_d = din("filt_w1", [33, 64])
    fw2_d = din("filt_w2", [64, 64])
    fw3_d = din("filt_w3", [64, 64])
    fb_d = din("filt_b", [64, 4])
    fw4_d = din("filt_w4", [64, 2048])
    hd_d = din("hyena_d", [1, D])
    who_d = din("w_hyena_out", [D, D])
    pw_d = din("pool_w", [4, 256, 256])
    pbs_d = din("pool_bs", [128, 16])
    wpo_d = din("w_pool_out", [D, D])
    wout_d = din("w_out", [D, D])
    gf_d = din("g_final", [1, D])
    cef_d = din("cef", [8, 128, 1024], BF16)
    cof_d = din("cof", [8, 128, 1024], BF16)
    sef_d = din("sef", [8, 128, 1024], BF16)
    sof_d = din("sof", [8, 128, 1024], BF16)
    cei_d = din("cei", [4, 128, 2048], BF16)
    sei_d = din("sei", [4, 128, 2048], BF16)
    coi_d = din("coi", [4, 128, 2048], BF16)
    soi_d = din("soi", [4, 128, 2048], BF16)
    ident_d = din("ident", [128, 128], BF16)
    zT_d = din("zT", [33, L])
    negt_d = din("negt", [128, 16])
    delt_d = din("deltas", [1, D])
    band_d = din("band", [128, 2560], BF16)
    out_d = nc.dram_tensor("out", [L, D], F32, kind="ExternalOutput").ap()
    dbg_d = None
    if dbg is not None:
        dbg_d = nc.dram_tensor("dbg", [128, dbg], F32, kind="ExternalOutput").ap()

    win_v = win_d.rearrange("(j p) n -> p j n", p=128)

    with ExitStack() as st:
        def sb(name, shape, dt):
            return st.enter_context(nc.sbuf_tensor("s_" + name, list(shape), dt))

        HT = sb("HT", [128, 16384], BF16)
        KA = sb("KA", [128, 16384], BF16)
        UT = sb("UT", [128, 8192], BF16)
        YA = sb("YA", [128, 16384], BF16)
        YH = sb("YH", [128, 16384], BF16)
        DF = sb("DF", [128, 12288], BF16)
        TM = sb("TM", [128, 8192], BF16)
        gn_sb = sb("gn", [128, 8], F32)
        bin_sb = sb("bin", [128, 64], F32)
        cw_sb = sb("cw", [128, 72], F32)
        cb_sb = sb("cb", [128, 24], F32)
        fw1_sb = sb("fw1", [33, 64], F32)
        fw2_sb = sb("fw2", [64, 64], F32)
        fw3_sb = sb("fw3", [64, 64], F32)
        fb_sb = sb("fb", [64, 4], F32)
        fbf_sb = sb("fbf", [64, 4], F32)
        h3T = sb("h3T", [64, L], F32)
        w4h = sb("w4h", [64, 1024], F32)
        dlt = sb("dlt", [128, 512], F32)
        dsb = sb("dsb", [128, 512], F32)
        ddb = sb("ddb", [33, 512], BF16)
        sel = sb("sel", [33, 128], BF16)
        negt_sb = sb("negt", [128, 16], F32)
        ident = sb("ident", [128, 128], BF16)
        pbs_sb = sb("pbs", [128, 16], F32)
        ss_sb = sb("ss", [128, 16], F32)
        rs_sb = sb("rs", [128, 16], F32)
        ss2_sb = sb("ss2", [128, 16], F32)
        rs2_sb = sb("rs2", [128, 16], F32)
        ps = st.enter_context(nc.psum_tensor("ps", [128, 4096], F32))

        epsc = sb("epsc", [128, 4], F32)
        S = Sched(nc)
        _DBG_SCHED['S'] = S

        def f(e):
            return e.memset(epsc[:, :], EPS)
        S.op("dve", f, writes=[("SM", "epsc")])

        def f(e):
            e.memset(ddb[:, :], 0.0)
            return e.memset(sel[:, :], 0.0)
        S.op("pool", f, writes=[("SM", "ddb"), ("SM", "sel")])

        def f(e):
            return e.memset(sel[0:1, :], 1.0)
        S.op("pool", f, writes=[("SM", "sel")])

        def f(e):
            return e.memset(sel[32:33, :], 1.0)
        S.op("pool", f, writes=[("SM", "sel")])

        def bank(b):
            return ps[:, b * 512:(b + 1) * 512]

        def bank_bf(b):
            return ps[:, b * 512:(b + 1) * 512].bitcast(BF16)

        def f32v(arena, off_bf, n_f32):
            return arena[:, off_bf:off_bf + 2 * n_f32].bitcast(F32)

        hT = HT[:, :].rearrange("p (j t) -> p j t", j=8)
        yh = YH[:, :].rearrange("p (j t) -> p j t", j=8)

        def mm_group(out_ap, pairs):
            def fn(e):
                n = len(pairs)
                last = None
                for i, (l, r) in enumerate(pairs):
                    last = e.matmul(out_ap, lhsT=l, rhs=r, start=(i == 0), stop=(i == n - 1))
                return last
            return fn

        def dbg_dump(ap_bf_or_f32, ncols, reads, nparts=128):
            S.dma("pool", [(dbg_d[0:nparts, 0:ncols], ap_bf_or_f32)], reads=reads, final=True)

        small = [(gn_sb[:, :], gn_d), (bin_sb[:, :], bin_d), (cw_sb[:, :], cw_d), (cb_sb[:, :], cb_d),
                 (fw1_sb[:, :], fw1_d), (fw2_sb[:, :], fw2_d), (fw3_sb[:, :], fw3_d), (fb_sb[:, :], fb_d),
                 (negt_sb[:, :], negt_d), (ident[:, :], ident_d), (pbs_sb[:, :], pbs_d)]
        S.dma("sp", small, writes=[("SM", "params")])

        def xs(a):
            ar = KA if a < 8 else YA
            return f32v(ar, (a % 8) * 2048, 1024)

        def xkey(a):
            return ("KA" if a < 8 else "YA", "x", a)

        junk = TM[:, 0:1024]

        def p1_square(a):
            def f(e):
                return e.activation(out=junk, in_=xs(a), func=AF.Square, accum_out=ss_sb[:, a:a + 1])
            S.op("act", f, reads=[xkey(a)], writes=[("SM", "ss", a), ("TM", "junk")])

        def p1_rstd():
            def f(e):
                return e.activation(out=rs_sb[:, :], in_=ss_sb[:, :], func=AF.Sqrt, bias=epsc[:, 0:1], scale=1.0 / D)
            S.op("act", f, reads=[("SM", "ss", a) for a in range(16)] + [("SM", "epsc")], writes=[("SM", "rs")])

            def f(e):
                return e.reciprocal(out=rs_sb[:, :], in_=rs_sb[:, :])
            S.op("dve", f, reads=[("SM", "rs")], writes=[("SM", "rs")])

        def p1_norm(a):
            xn = TM[:, 2048 + (a % 2) * 1024: 2048 + (a % 2 + 1) * 1024]
            xnk = ("TM", "xn", a % 2)

            def f(e):
                return e.tensor_scalar(out=xn, in0=xs(a), scalar1=rs_sb[:, a:a + 1], scalar2=None, op0=ALU.mult)
            S.op("dve", f, reads=[xkey(a), ("SM", "rs")], writes=[xnk])

        def p1_chunk(a):
            xn = TM[:, 2048 + (a % 2) * 1024: 2048 + (a % 2 + 1) * 1024]
            xnk = ("TM", "xn", a % 2)
            b = a % 2
            pk = ("PS", b)

            def f(e):
                last = None
                for j in range(8):
                    last = e.transpose(out=bank_bf(b)[:, j * 128:(j + 1) * 128], in_=xn[:, j * 128:(j + 1) * 128],
                                       identity=ident[:, :])
                return last
            S.op("pe", f, reads=[xnk, ("SM", "params")], writes=[pk])

            def f(e):
                return e.tensor_tensor(out=hT[:, :, a * 128:(a + 1) * 128],
                                       in0=bank_bf(b).rearrange("p (j t) -> p j t", j=8),
                                       in1=gn_sb[:, 0:8].unsqueeze(2).to_broadcast([128, 8, 128]),
                                       op=ALU.mult)
            S.op("dve", f, reads=[pk, ("SM", "params")], writes=[("HT", "c", a)])
        HTALL = [("HT", "c", a) for a in range(16)]

        zT = UT[:, 0:4096].bitcast(F32)[0:33, :]
        hA = DF[:, 0:4096].bitcast(F32)[0:64, :]
        hB = DF[:, 4096:8192].bitcast(F32)[0:64, :]
        S.dma("sp", [(zT, zT_d)], writes=[("UT", "zT")])
        for a in range(16):
            S.dma("sp", [(xs(a), x_d[a * 128:(a + 1) * 128, :])], writes=[xkey(a)])

        def f(e):
            return e.tensor_scalar(out=fbf_sb[:, 0:3], in0=fb_sb[:, 0:3], scalar1=fb_sb[:, 3:4], scalar2=None,
                                   op0=ALU.mult)
        S.op("dve", f, reads=[("SM", "params")], writes=[("SM", "fbf")])
        layers = [(fw1_sb, 33, zT, ("UT", "zT"), hA, ("DF", "hA")),
                  (fw2_sb, 64, hA, ("DF", "hA"), hB, ("DF", "hB")),
                  (fw3_sb, 64, hB, ("DF", "hB"), h3T[:, :], ("SM", "h3T"))]

        def filt_step(li, tb):
            wsb, kdim, src, srck, dst, dstk = layers[li]
            b = tb % 2
            t1 = DF[:, 8192 + b * 1024: 8192 + (b + 1) * 1024].bitcast(F32)[0:64, :]
            t2 = DF[:, 10240 + b * 1024: 10240 + (b + 1) * 1024].bitcast(F32)[0:64, :]
            t1k, t2k = ("DF", "t1", b), ("DF", "t2", b)
            pso = ps[0:64, (2 + b) * 512:(3 + b) * 512]
            S.op("pe", mm_group(pso, [(wsb[0:kdim, 0:64], src[0:kdim, tb * 512:(tb + 1) * 512])]),
                 reads=[srck, ("SM", "params")], writes=[("PS", 2 + b)])

            def f(e):
                return e.activation(out=t1, in_=pso, func=AF.Identity, bias=fbf_sb[:, li:li + 1],
                                    scale=fb_sb[:, 3:4])
            S.op("act", f, reads=[("PS", 2 + b), ("SM", "fbf"), ("SM", "params")], writes=[t1k])

            def f(e):
                return e.tensor_scalar(out=t2, in0=t1, scalar1=1.0 / TWO_PI, scalar2=MAGIC,
                                       op0=ALU.mult, op1=ALU.add)
            S.op("dve", f, reads=[t1k], writes=[t2k])

            def f(e):
                return e.tensor_scalar(out=t2, in0=t2, scalar1=-MAGIC, scalar2=-TWO_PI,
                                       op0=ALU.add, op1=ALU.mult)
            S.op("dve", f, reads=[t2k], writes=[t2k])

            def f(e):
                return e.tensor_tensor(out=t1, in0=t1, in1=t2, op=ALU.add)
            S.op("dve", f, reads=[t1k, t2k], writes=[t1k])

            def f(e):
                return e.tensor_scalar(out=t1, in0=t1, scalar1=-PI_SAFE, scalar2=PI_SAFE,
                                       op0=ALU.max, op1=ALU.min)
            S.op("dve", f, reads=[t1k], writes=[t1k])

            def f(e):
                return e.activation(out=dst[:, tb * 512:(tb + 1) * 512], in_=t1, func=AF.Sin)
            S.op("act", f, reads=[t1k], writes=[dstk])

        fsteps = [(li, tb) for li in range(3) for tb in range(4)] if stage >= 2 else []
        fi = 0
        for a in range(16):
            if a % 4 == 0 and fi < len(fsteps):
                filt_step(*fsteps[fi])
                fi += 1
            p1_square(a)
        p1_rstd()
        p1_norm(0)
        for a in range(16):
            if a % 2 == 0 and fi < len(fsteps):
                filt_step(*fsteps[fi])
                fi += 1
            if a + 1 < 16:
                p1_norm(a + 1)
            p1_chunk(a)
        while fi < len(fsteps):
            filt_step(*fsteps[fi])
            fi += 1
        if stage == 1:
            dbg_dump(HT[:, :], 16384, HTALL)
        if stage == 2:
            dbg_dump(h3T[:, :], 2048, [("SM", "h3T")], nparts=64)
        S.recycle("KA")
        S.recycle("YA")
        S.recycle("TM")
        S.recycle("DF")
        S.recycle("UT")

        SC = 2.0 / NFFT
        kE = KA[:, 0:8192].rearrange("p (a c) -> p a c", a=16)
        kO = KA[:, 8192:16384].rearrange("p (a c) -> p a c", a=16)
        uT = UT[:, :].rearrange("p (a c) -> p a c", a=16)
        Xe = YA[:, 0:4096].rearrange("p (a c) -> p a c", a=8)
        Ze = YA[:, 4096:8192].rearrange("p (a c) -> p a c", a=8)
        Xo = YA[:, 8192:12288].rearrange("p (a c) -> p a c", a=8)
        Zo = YA[:, 12288:16384].rearrange("p (a c) -> p a c", a=8)

        def tmf(i):
            return TM[:, i * 1024:(i + 1) * 1024].bitcast(F32)

        def conv_steps(pb, pbk, acc, acck, chunk, first_on_dve=False):
            if first_on_dve:
                def f(e):
                    return e.tensor_scalar(out=acc, in0=pb[:, 0:256], scalar1=cw_sb[:, chunk:chunk + 1],
                                           scalar2=cb_sb[:, chunk:chunk + 1], op0=ALU.mult, op1=ALU.add)
                S.op("dve", f, reads=[pbk, ("SM", "params")], writes=[acck])
            else:
                def f(e):
                    return e.activation(out=acc, in_=pb[:, 0:256], func=AF.Identity,
                                        bias=cb_sb[:, chunk:chunk + 1], scale=cw_sb[:, chunk:chunk + 1])
                S.op("act", f, reads=[pbk, ("SM", "params")], writes=[acck])
            for k in (1, 2):
                def f(e, k=k):
                    return e.scalar_tensor_tensor(out=acc, in0=pb[:, k:k + 256],
                                                  scalar=cw_sb[:, 24 * k + chunk:24 * k + chunk + 1],
                                                  in1=acc, op0=ALU.mult, op1=ALU.add)
                S.op("dve", f, reads=[pbk, acck, ("SM", "params")], writes=[acck])

        def proj_conv_in(psb, wslab, wkey, cc, chunk, tb, pb, pbk):
            t0 = tb * 256
            lo, hi = max(t0 - 1, 0), min(t0 + 257, L)
            W = hi - lo
            off = lo - (t0 - 1)
            pso = bank(psb)[:, 0:W]
            S.op("pe", mm_group(pso, [(wslab[:, j, cc * 128:(cc + 1) * 128], hT[:, j, lo:hi]) for j in range(8)]),
                 reads=HTALL + [wkey], writes=[("PS", psb)])

            def f(e):
                if tb == 0:
                    e.memzero(pb[:, 0:1])
                if tb == 7:
                    e.memzero(pb[:, 257:258])
                return e.activation(out=pb[:, off:off + W], in_=pso, func=AF.Identity,
                                    bias=bin_sb[:, chunk:chunk + 1], scale=1.0)
            S.op("act", f, reads=[("PS", psb), ("SM", "params")], writes=[pbk])

        who_v = who_d.rearrange("(j p) n -> p j n", p=128)
        wpo_v = wpo_d.rearrange("(j p) n -> p j n", p=128)
        wout_v = wout_d.rearrange("(j p) n -> p j n", p=128)

        def cols(v, c0):
            return v[:, :, c0:c0 + 512]
        slab_srcs = [cols(who_v, 0), cols(win_v, 6144), cols(who_v, 512), cols(win_v, 6144 + 512),
                     cols(win_v, 4096), cols(win_v, 4096 + 512),
                     cols(win_v, 5120), cols(win_v, 5120 + 512),
                     cols(wpo_v, 0), cols(win_v, 7168), cols(wpo_v, 512), cols(win_v, 7168 + 512),
                     cols(wout_v, 0), cols(wout_v, 512)]
        slab_issued = [0]
        slab_next = [0]

        def slab_slot(i):
            sl = i % 4
            if sl < 3:
                return DF[:, sl * 4096:(sl + 1) * 4096], ("DF", "ws", sl)
            return TM[:, 4096:8192], ("TM", "ws", 3)

        def slab_issue_upto(n):
            while slab_issued[0] < min(n, len(slab_srcs)):
                i = slab_issued[0]
                slab_issued[0] += 1
                flat, key = slab_slot(i)
                src = slab_srcs[i]
                if isinstance(src, str):
                    pwt_ = flat[:, 0:2048].rearrange("p (g c d) -> p g c d", g=4, c=2)
                    S.dma("pool", [(pwt_[:, g, :, :], pw_d[g].rearrange("(c p) d -> p c d", p=128)) for g in range(4)],
                          writes=[key])
                else:
                    S.dma("pool", [(flat.rearrange("p (j n) -> p j n", j=8), src)], writes=[key])

        def load_slab(_unused=None):
            i = slab_next[0]
            slab_next[0] += 1
            slab_issue_upto(i + 3)
            flat, key = slab_slot(i)
            return flat.rearrange("p (j n) -> p j n", j=8), key

        n_half = 2 if stage >= 3 else 0
        for h in range(n_half):
            S.dma("sp", [(w4h[:, 0:512], fw4_d[:, h * 512:(h + 1) * 512]),
                         (w4h[:, 512:1024], fw4_d[:, 1024 + h * 512:1024 + (h + 1) * 512]),
                         (dlt[:, :], delt_d[:, h * 512:(h + 1) * 512].partition_broadcast(128)[:, 0, :]),
                         (dsb[0:1, :], hd_d[:, h * 512:(h + 1) * 512]),
                         (dsb[32:33, :], hd_d[:, h * 512:(h + 1) * 512])],
                  writes=[("SM", "w4h"), ("SM", "dlt"), ("SM", "dsb")])

            def f(e):
                return e.tensor_copy(out=ddb[0:1, :], in_=dsb[0:1, :])
            S.op("dve", f, reads=[("SM", "dsb")], writes=[("SM", "ddb")])

            def f(e):
                return e.tensor_copy(out=ddb[32:33, :], in_=dsb[32:33, :])
            S.op("dve", f, reads=[("SM", "dsb"), ("SM", "ddb")], writes=[("SM", "ddb")])

            def f(e):
                return e.tensor_tensor(out=ddb[32:33, :], in0=dsb[32:33, :], in1=ddb[32:33, :], op=ALU.subtract)
            S.op("dve", f, reads=[("SM", "dsb"), ("SM", "ddb")], writes=[("SM", "ddb")])

            def f(e):
                return e.tensor_tensor(out=w4h[:, 0:512], in0=w4h[:, 0:512], in1=w4h[:, 512:1024], op=ALU.add)
            S.op("dve", f, reads=[("SM", "w4h")], writes=[("SM", "w4h")])

            def f(e):
                return e.scalar_tensor_tensor(out=w4h[:, 512:1024], in0=w4h[:, 512:1024], scalar=-2.0,
                                              in1=w4h[:, 0:512], op0=ALU.mult, op1=ALU.add)
            S.op("dve", f, reads=[("SM", "w4h")], writes=[("SM", "w4h")])

            wx1 = YA[:, 0:4096].rearrange("p (j n) -> p j n", j=8)
            wv = YA[:, 4096:8192].rearrange("p (j n) -> p j n", j=8)
            S.dma("pool", [(wx1, win_v[:, :, 1024 + h * 512:1024 + (h + 1) * 512])], writes=[("YA", "wx1")])
            S.dma("pool", [(wv, win_v[:, :, 2048 + h * 512:2048 + (h + 1) * 512])], writes=[("YA", "wv")])
            for tc in range(16):
                b = tc % 2
                dec = tmf(b)
                deck = ("TM", "dec", b)
                pf, pbk_ = 2 * b, 2 * b + 1
                kr, kmc = divmod(tc, 8)
                hcols = h3T[0:64, 256 * kmc + kr:256 * (kmc + 1):2]
                S.op("pe", mm_group(bank(pf), [(hcols, w4h[0:64, 0:512])]),
                     reads=[("SM", "h3T"), ("SM", "w4h")], writes=[("PS", pf)])
                S.op("pe", mm_group(bank(pbk_), [(hcols, w4h[0:64, 512:1024])]),
                     reads=[("SM", "h3T"), ("SM", "w4h")], writes=[("PS", pbk_)])

                def f(e, dec=dec, tc=tc):
                    return e.activation(out=dec, in_=dlt[:, :], func=AF.Exp, scale=negt_sb[:, tc:tc + 1])
                S.op("act", f, reads=[("SM", "dlt"), ("SM", "params")], writes=[deck])

                def f(e, dec=dec, pf=pf, tc=tc):
                    return e.tensor_tensor(out=kE[:, tc, :], in0=bank(pf), in1=dec, op=ALU.mult)
                S.op("dve", f, reads=[("PS", pf), deck], writes=[("KA", "kE", tc)])

                def f(e, dec=dec, pbk_=pbk_, tc=tc):
                    return e.tensor_tensor(out=kO[:, tc, :], in0=bank(pbk_), in1=dec, op=ALU.mult)
                S.op("dve", f, reads=[("PS", pbk_), deck], writes=[("KA", "kO", tc)])
            KALL = [("KA", "kE", tc) for tc in range(16)] + [("KA", "kO", tc) for tc in range(16)]
            if stage == 3 and h == 0:
                dbg_dump(KA[:, :], 16384, KALL)
            S.recycle("TM")
            if stage == 3:
                break

            pbx = DF[:, 0:4112].bitcast(F32)
            pbv = DF[:, 4112:8224].bitcast(F32)
            utfs = [YA[:, 8192 + p * 2048: 8192 + (p + 1) * 2048] for p in range(2)]
            ax = TM[:, 0:4096].bitcast(F32)
            av = TM[:, 4096:8192].bitcast(F32)
            kpx, kpv, kax, kav = ("DF", "pbx"), ("DF", "pbv"), ("TM", "ax"), ("TM", "av")

            def f(e):
                e.memset(pbx[:, 0:1], 0.0)
                e.memset(pbx[:, 2049:2050], 0.0)
                e.memset(pbv[:, 0:1], 0.0)
                return e.memset(pbv[:, 2049:2050], 0.0)
            S.op("dve", f, writes=[("DF", "pads")])
            ubank = [0]

            def u_proj(wslab, wkey, cc, chunk, pb, pbkey, bank0):
                for j in range(4):
                    b = bank0 + ubank[0] % 3
                    ubank[0] += 1
                    S.op("pe", mm_group(bank(b), [(wslab[:, jd, cc * 128:(cc + 1) * 128], hT[:, jd, j * 512:(j + 1) * 512])
                                                  for jd in range(8)]),
                         reads=HTALL + [wkey], writes=[("PS", b)])

                    def f(e, b=b, j=j):
                        return e.activation(out=pb[:, 1 + 512 * j:1 + 512 * (j + 1)], in_=bank(b), func=AF.Identity,
                                            bias=bin_sb[:, chunk:chunk + 1], scale=1.0)
                    S.op("act", f, reads=[("PS", b), ("SM", "params")], writes=[pbkey])

            def u_conv(pb, pbkey, acc, acck, chunk, first_on_dve):
                rd = [pbkey, ("DF", "pads"), ("SM", "params")]
                if first_on_dve:
                    def f(e):
                        return e.tensor_scalar(out=acc, in0=pb[:, 0:2048], scalar1=cw_sb[:, chunk:chunk + 1],
                                               scalar2=cb_sb[:, chunk:chunk + 1], op0=ALU.mult, op1=ALU.add)
                    S.op("dve", f, reads=rd, writes=[acck])
                else:
                    def f(e):
                        return e.activation(out=acc, in_=pb[:, 0:2048], func=AF.Identity,
                                            bias=cb_sb[:, chunk:chunk + 1], scale=cw_sb[:, chunk:chunk + 1])
                    S.op("act", f, reads=rd, writes=[acck])
                for k in (1, 2):
                    def f(e, k=k):
                        return e.scalar_tensor_tensor(out=acc, in0=pb[:, k:k + 2048],
                                                      scalar=cw_sb[:, 24 * k + chunk:24 * k + chunk + 1],
                                                      in1=acc, op0=ALU.mult, op1=ALU.add)
                    S.op("dve", f, reads=rd + [acck], writes=[acck])

            def u_stage_a(cc):
                gc = 4 * h + cc
                u_proj(wx1, ("YA", "wx1"), cc, 8 + gc, pbx, kpx, 0)
                u_conv(pbx, kpx, ax, kax, 8 + gc, False)
                u_proj(wv, ("YA", "wv"), cc, 16 + gc, pbv, kpv, 3)
                u_conv(pbv, kpv, av, kav, 16 + gc, True)

                utf, kut = utfs[cc % 2], ("YA", "ut", cc % 2)

                def f(e):
                    return e.tensor_tensor(out=utf, in0=ax, in1=av, op=ALU.mult)
                S.op("dve", f, reads=[kax, kav], writes=[kut])

            def u_stage_b(cc):
                utf, kut = utfs[cc % 2], ("YA", "ut", cc % 2)
                for q in range(4):
                    pT = 6 + q % 2

                    def f(e, q=q, pT=pT):
                        last = None
                        for k in range(4):
                            ur, umc = divmod(4 * q + k, 8)
                            last = e.transpose(out=bank_bf(pT)[:, k * 128:(k + 1) * 128],
                                               in_=utf[:, 256 * umc + ur:256 * (umc + 1):2], identity=ident[:, :])
                        return last
                    S.op("pe", f, reads=[kut, ("SM", "params")], writes=[("PS", pT)])

                    def f(e, q=q, pT=pT):
                        return e.activation(out=uT[:, 4 * q:4 * q + 4, cc * 128:(cc + 1) * 128],
                                            in_=bank_bf(pT)[:, 0:512].rearrange("p (k c) -> p k c", k=4),
                                            func=AF.Copy)
                    S.op("act", f, reads=[("PS", pT)], writes=[("UT", "uT", 4 * q + k) for k in range(4)])

            KALL = [("KA", "kE", tc) for tc in range(16)] + [("KA", "kO", tc) for tc in range(16)]
            UTALL = [("UT", "uT", a) for a in range(16)]
            mats = (cef_d, cof_d, sef_d, sof_d)
            slabs0 = []
            for m in range(4):
                ap0 = YH[:, 8192 + m * 1024: 8192 + (m + 1) * 1024]
                S.dma("sp", [(ap0, mats[m][0])], writes=[("YH", "dfs", m)])
                slabs0.append((ap0.rearrange("p (a f) -> p a f", a=8), ("YH", "dfs", m)))

            def fwd_kgroup(slabs):
                for bnk, mi, src, par in ((0, 0, kE, 0), (1, 1, kE, 1), (2, 2, kO, 0), (3, 3, kO, 1)):
                    pairs = [(slabs[mi][0][:, mc, :], src[:, 8 * par + mc, :]) for mc in range(8)]
                    if bnk == 0:
                        pairs.append((sel[0:33, :], ddb[0:33, :]))
                    S.op("pe", mm_group(bank(bnk), pairs),
                         reads=KALL + [slabs[mi][1], ("SM", "ddb"), ("SM", "sel")], writes=[("PS", bnk)])

            def fwd_agroup(slabs):
                for bnk, mi, par in ((4, 0, 0), (5, 1, 1), (6, 2, 0), (7, 3, 1)):
                    S.op("pe", mm_group(bank(bnk), [(slabs[mi][0][:, mc, :], uT[:, 8 * par + mc, :]) for mc in range(8)]),
                         reads=UTALL + [slabs[mi][1]], writes=[("PS", bnk)])

            def fwd_pq0():
                fwd_kgroup(slabs0)

            u_stage_a(0)
            for cc in range(4):
                if cc + 1 < 4:
                    u_stage_a(cc + 1)
                else:
                    fwd_pq0()
                u_stage_b(cc)
            if stage == 4 and h == 0:
                dbg_dump(UT[:, :], 8192, UTALL)
            S.recycle("TM")
            S.recycle("YA")
            S.recycle("DF")
            if stage == 4:
                break

            def ftile(i):
                if i < 8:
                    return tmf(i), ("TM", "T", i)
                return DF[:, 6144 + (i - 8) * 1024: 6144 + (i - 7) * 1024].bitcast(F32), ("DF", "T", i)

            for fc in range(8):
                slabs = []
                for m in range(4):
                    if fc == 0:
                        slabs.append(slabs0[m])
                        continue
                    sl = (4 * (fc - 1) + m) % 6
                    ap = DF[:, sl * 1024:(sl + 1) * 1024]
                    skey = ("DF", "slab", sl)
                    S.dma("sp", [(ap, mats[m][fc])], writes=[skey])
                    slabs.append((ap.rearrange("p (a f) -> p a f", a=8), skey))
                if fc > 0:
                    fwd_kgroup(slabs)
                fwd_agroup(slabs)
                if fc == 0:
                    S.recycle("YH")
                T = [ftile(i) for i in range(14)]
                PSk = [("PS", b) for b in range(8)]

                def act_copy(dst, b, scale):
                    def f(e):
                        return e.activation(out=T[dst][0], in_=bank(b), func=AF.Copy, scale=scale)
                    S.op("act", f, reads=[PSk[b]], writes=[T[dst][1]])

                def tt(eng, dst_ap, dst_key, a, b, op):
                    def res(x):
                        if x[0] == 'T':
                            return T[x[1]][0], T[x[1]][1]
                        if x[0] == 'P':
                            return bank(x[1]), PSk[x[1]]
                        return x[1], x[2]
                    (a_ap, a_k), (b_ap, b_k) = res(a), res(b)

                    def f(e):
                        return e.tensor_tensor(out=dst_ap, in0=a_ap, in1=b_ap, op=op)
                    S.op(eng, f, reads=[a_k, b_k], writes=[dst_key])

                def stt(dst, b, scalar, src, op1=ALU.add):
                    def f(e):
                        return e.scalar_tensor_tensor(out=T[dst][0], in0=bank(b), scalar=scalar, in1=T[src][0],
                                                      op0=ALU.mult, op1=op1)
                    S.op("dve", f, reads=[PSk[b], T[src][1]], writes=[T[dst][1]])
                def dbl(i):
                    if i < 8:
                        return TM[:, i * 1024:(i + 2) * 1024].bitcast(F32)
                    return DF[:, 6144 + (i - 8) * 1024: 6144 + (i - 6) * 1024].bitcast(F32)

                def ttd(dst, a, b, op):
                    def f(e):
                        return e.tensor_tensor(out=dbl(dst), in0=dbl(a), in1=dbl(b), op=op)
                    S.op("dve", f, reads=[T[a][1], T[a + 1][1], T[b][1], T[b + 1][1]],
                         writes=[T[dst][1], T[dst + 1][1]])
                act_copy(1, 1, SC)
                act_copy(3, 3, SC)
                stt(0, 0, SC, 1)
                stt(1, 0, SC, 1, ALU.subtract)
                stt(2, 2, SC, 3)
                stt(3, 2, -SC, 3)
                act_copy(5, 5, 1.0)
                act_copy(7, 7, 1.0)
                tt("dve", T[4][0], T[4][1], ('P', 4), ('T', 5), ALU.add)
                tt("dve", T[5][0], T[5][1], ('P', 4), ('T', 5), ALU.subtract)
                tt("dve", T[6][0], T[6][1], ('P', 6), ('T', 7), ALU.add)
                tt("dve", T[7][0], T[7][1], ('T', 7), ('P', 6), ALU.subtract)
                ttd(8, 4, 0, ALU.mult)
                ttd(10, 6, 2, ALU.mult)
                ttd(8, 8, 10, ALU.subtract)
                ttd(2, 4, 2, ALU.mult)
                ttd(0, 6, 0, ALU.mult)
                ttd(2, 2, 0, ALU.add)
                tt("dve", Xe[:, fc, :], ("YA", "Xe", fc), ('T', 8), ('T', 9), ALU.add)
                tt("dve", Ze[:, fc, :], ("YA", "Ze", fc), ('T', 2), ('T', 3), ALU.subtract)
                tt("dve", Xo[:, fc, :], ("YA", "Xo", fc), ('T', 8), ('T', 9), ALU.subtract)
                tt("dve", Zo[:, fc, :], ("YA", "Zo", fc), ('T', 2), ('T', 3), ALU.add)
            YALL = [("YA", n, fc) for n in ("Xe", "Ze", "Xo", "Zo") for fc in range(8)]
            if stage == 5 and h == 0:
                dbg_dump(YA[:, :], 16384, YALL)
            S.recycle("TM")
            S.recycle("KA")
            S.recycle("UT")
            S.recycle("DF")
            S.recycle("YH")
            if stage == 5:
                break

            wx0 = UT[:, 0:4096].rearrange("p (j n) -> p j n", j=8)
            wz = UT[:, 4096:8192].rearrange("p (j n) -> p j n", j=8)
            kwx0, kwz = ("UT", "wx0"), ("UT", "wz")
            S.dma("pool", [(wx0, win_v[:, :, h * 512:(h + 1) * 512])], writes=[kwx0])
            S.dma("pool", [(wz, win_v[:, :, 3072 + h * 512:3072 + (h + 1) * 512])], writes=[kwz])
            if h == 1 and stage >= 7:
                slab_issue_upto(3)
            step = 0
            sub = 0
            for tb5 in range(4):
                sl = tb5 % 2
                base = sl * 8192
                islabs = []
                for mi, src in enumerate((cei_d, sei_d, coi_d, soi_d)):
                    ap = KA[:, base + mi * 2048: base + (mi + 1) * 2048]
                    islabs.append(ap.rearrange("p (a t) -> p a t", a=8))
                S.dma("sp", [(KA[:, base + mi * 2048: base + (mi + 1) * 2048], src[tb5])
                             for mi, src in enumerate((cei_d, sei_d, coi_d, soi_d))], writes=[("KA", "inv", sl)])
                for cc in range(4):
                    gc = 4 * h + cc
                    p = step % 2
                    step += 1
                    by = p
                    cs = slice(cc * 128, (cc + 1) * 128)

                    def f(e, by=by, cs=cs, islabs=islabs):
                        last = None
                        for half_, (xa, za, ci, si) in enumerate(((Xe, Ze, 0, 1), (Xo, Zo, 2, 3))):
                            o = bank(by)[:, half_ * 256:(half_ + 1) * 256]
                            n_ = 16
                            i_ = 0
                            for fc in range(8):
                                for src, mi in ((xa, ci), (za, si)):
                                    last = e.matmul(o, lhsT=src[:, fc, cs], rhs=islabs[mi][:, fc, :],
                                                    start=(i_ == 0), stop=(i_ == n_ - 1))
                                    i_ += 1
                        return last
                    S.op("pe", f, reads=YALL + [("KA", "inv", sl)], writes=[("PS", by)])
                    for sb_ in range(2):
                        tb = 2 * tb5 + sb_
                        t0 = tb * 256
                        q = sub % 2
                        sub += 1
                        bx, bz = 2 + q, 4 + q

                        def slot(i):
                            return TM[:, i * 544:(i + 1) * 544].bitcast(F32)
                        pb0, a0, sz, gg = slot(q), slot(2 + q)[:, 0:256], slot(4 + q)[:, 0:256], slot(6 + q)[:, 0:256]
                        k0, ka0, ksz, kgg = (("TM", n, q) for n in ("pb0", "a0", "sz", "gg"))
                        proj_conv_in(bx, wx0, kwx0, cc, gc, tb, pb0, k0)
                        S.op("pe", mm_group(bank(bz)[:, 0:256],
                                            [(wz[:, j, cs], hT[:, j, t0:t0 + 256]) for j in range(8)]),
                             reads=HTALL + [kwz], writes=[("PS", bz)])

                        def f(e, sz=sz, bz=bz, gc=gc):
                            return e.activation(out=sz, in_=bank(bz)[:, 0:256], func=AF.Silu,
                                                bias=bin_sb[:, 24 + gc:25 + gc], scale=1.0)
                        S.op("act", f, reads=[("PS", bz), ("SM", "params")], writes=[ksz])
                        conv_steps(pb0, k0, a0, ka0, gc)

                        def f(e, gg=gg, a0=a0, sz=sz):
                            return e.tensor_tensor(out=gg, in0=a0, in1=sz, op=ALU.mult)
                        S.op("dve", f, reads=[ka0, ksz], writes=[kgg])

                        def f(e, by=by, gg=gg, gc=gc, t0=t0, sb_=sb_):
                            last = None
                            for r in range(2):
                                last = e.tensor_tensor(out=yh[:, gc, t0 + r:t0 + 256:2],
                                                       in0=bank(by)[:, r * 256 + sb_ * 128: r * 256 + (sb_ + 1) * 128],
                                                       in1=gg[:, r:256:2], op=ALU.mult)
                            return last
                        S.op("dve", f, reads=[("PS", by), kgg], writes=[("YH", "c", gc)])
            S.recycle("TM")
            S.recycle("KA")
            S.recycle("UT")
            S.recycle("YA")
            if h == 0:
                S.recycle("YH")
        YHALL = [("YH", "c", gc) for gc in range(8)]
        if stage == 6:
            dbg_dump(YH[:, :], 16384, YHALL)

        if stage >= 7:
            m1 = KA[:, :].rearrange("p (j t) -> p j t", j=8)
            mg = YA[:, :].rearrange("p (j t) -> p j t", j=8)

            def branch_out(wmat_v, gate_col0, src3, srckeys, combine):
                step = 0
                for half in range(2):
                    wa, wak = load_slab(wmat_v[:, :, half * 512:(half + 1) * 512])
                    wg, wgk = load_slab(win_v[:, :, gate_col0 + half * 512:gate_col0 + (half + 1) * 512])
                    for dcl in range(4):
                        dc = 4 * half + dcl
                        for tb in range(4):
                            p = step % 2
                            step += 1
                            bo, bg = 2 * p, 2 * p + 1
                            ts = slice(tb * 512, (tb + 1) * 512)
                            S.op("pe", mm_group(bank(bo), [(wa[:, j, dcl * 128:(dcl + 1) * 128], src3[:, j, ts])
                                                           for j in range(8)]),
                                 reads=srckeys + [wak], writes=[("PS", bo)])
                            S.op("pe", mm_group(bank(bg), [(wg[:, j, dcl * 128:(dcl + 1) * 128], hT[:, j, ts])
                                                           for j in range(8)]),
                                 reads=HTALL + [wgk], writes=[("PS", bg)])
                            gt = tmf(p)
                            gk = ("TM", "gt", p)
                            chunk = gate_col0 // 128 + dc

                            def f(e, gt=gt, bg=bg, chunk=chunk):
                                return e.activation(out=gt, in_=bank(bg), func=AF.Sigmoid,
                                                    bias=bin_sb[:, chunk:chunk + 1], scale=1.0)
                            S.op("act", f, reads=[("PS", bg), ("SM", "params")], writes=[gk])
                            combine(dc, ts, bo, gt, gk, p)

            def comb_h(dc, ts, bo, gt, gk, p):
                def f(e):
                    return e.tensor_tensor(out=m1[:, dc, ts], in0=bank(bo), in1=gt, op=ALU.mult)
                S.op("dve", f, reads=[("PS", bo), gk], writes=[("KA", "m1", dc)])
            branch_out(who_v, 6144, yh, YHALL, comb_h)
            M1ALL = [("KA", "m1", dc) for dc in range(8)]
            if stage == 7:
                dbg_dump(KA[:, :], 16384, M1ALL)
            S.recycle("YH")
            S.recycle("TM")

        if stage >= 8:
            plT = YH[:, :].rearrange("p (a c) -> p a c", a=16)
            for half in range(2):
                wp, wpk = load_slab(win_v[:, :, 4096 + half * 512:4096 + (half + 1) * 512])
                for tc in range(16):
                    b = tc % 2
                    S.op("pe", mm_group(bank(b), [(hT[:, j, tc * 128:(tc + 1) * 128], wp[:, j, :]) for j in range(8)]),
                         reads=HTALL + [wpk], writes=[("PS", b)])

                    def f(e, b=b, tc=tc, half=half):
                        return e.activation(out=plT[:, tc, half * 512:(half + 1) * 512], in_=bank(b), func=AF.Copy)
                    S.op("act", f, reads=[("PS", b)], writes=[("YH", "pl", tc)])
            PLALL = [("YH", "pl", tc) for tc in range(16)]
            band = UT[:, 0:2560].rearrange("p (k t) -> p k t", k=20)
            S.dma("sp", [(UT[:, 0:2560], band_d)], writes=[("UT", "band")])
            step = 0
            for gc in range(8):
                g = gc // 2
                for tq in range(4):
                    b = 2 + step % 2
                    step += 1

                    def f(e, gc=gc, g=g, tq=tq, b=b):
                        last = None
                        for a in range(4 * tq, 4 * tq + 4):
                            srcs = []
                            if a > 0:
                                srcs.append((a - 1, 3))
                            srcs.append((a, 0 if a == 0 else (2 if a == 15 else 1)))
                            if a < 15:
                                srcs.append((a + 1, 4))
                            o = bank(b)[:, (a - 4 * tq) * 128:(a - 4 * tq + 1) * 128]
                            for i, (sa, kind) in enumerate(srcs):
                                last = e.matmul(o, lhsT=plT[:, sa, gc * 128:(gc + 1) * 128], rhs=band[:, 5 * g + kind, :],
                                                start=(i == 0), stop=(i == len(srcs) - 1))
                        return last
                    S.op("pe", f, reads=PLALL + [("UT", "band")], writes=[("PS", b)])

                    def f(e, gc=gc, tq=tq, b=b):
                        return e.activation(out=mg[:, gc, tq * 512:(tq + 1) * 512], in_=bank(b), func=AF.Copy)
                    S.op("act", f, reads=[("PS", b)], writes=[("YA", "pooled", gc)])
            POALL = [("YA", "pooled", gc) for gc in range(8)]
            if stage == 8:
                dbg_dump(YA[:, :], 16384, POALL)
            S.recycle("YH")

        if stage >= 9:
            yp = YH[:, :].rearrange("p (j t) -> p j t", j=8)
            pwt = UT[:, 5120:7168].rearrange("p (g c d) -> p g c d", g=4, c=2)
            pwk = ("UT", "pw")
            S.dma("pool", [(pwt[:, g, :, :], pw_d[g].rearrange("(c p) d -> p c d", p=128)) for g in range(4)],
                  writes=[pwk])
            step = 0
            for half in range(2):
                wzp, wzk = load_slab(win_v[:, :, 5120 + half * 512:5120 + (half + 1) * 512])
                for dcl in range(4):
                    dc = 4 * half + dcl
                    g, dl = dc // 2, dc % 2
                    for tb in range(4):
                        p = step % 2
                        step += 1
                        bo, bz = 2 * p, 2 * p + 1
                        ts = slice(tb * 512, (tb + 1) * 512)
                        S.op("pe", mm_group(bank(bo), [(pwt[:, g, cl, dl * 128:(dl + 1) * 128], mg[:, 2 * g + cl, ts])
                                                       for cl in range(2)]),
                             reads=POALL + [pwk], writes=[("PS", bo)])
                        S.op("pe", mm_group(bank(bz), [(wzp[:, j, dcl * 128:(dcl + 1) * 128], hT[:, j, ts])
                                                       for j in range(8)]),
                             reads=HTALL + [wzk], writes=[("PS", bz)])
                        szp, ty = tmf(p), tmf(2 + p)
                        kszp, kty = ("TM", "szp", p), ("TM", "ty", p)

                        def f(e, szp=szp, bz=bz, dc=dc):
                            return e.activation(out=szp, in_=bank(bz), func=AF.Silu,
                                                bias=bin_sb[:, 40 + dc:41 + dc], scale=1.0)
                        S.op("act", f, reads=[("PS", bz), ("SM", "params")], writes=[kszp])

                        def f(e, ty=ty, bo=bo, dc=dc):
                            return e.tensor_scalar(out=ty, in0=bank(bo), scalar1=pbs_sb[:, dc:dc + 1],
                                                   scalar2=pbs_sb[:, 8 + dc:9 + dc], op0=ALU.add, op1=ALU.mult)
                        S.op("dve", f, reads=[("PS", bo), ("SM", "params")], writes=[kty])

                        def f(e, ty=ty, szp=szp, dc=dc, ts=ts):
                            return e.tensor_tensor(out=yp[:, dc, ts], in0=ty, in1=szp, op=ALU.mult)
                        S.op("dve", f, reads=[kty, kszp], writes=[("YH", "yp", dc)])
            YPALL = [("YH", "yp", dc) for dc in range(8)]
            if stage == 9:
                dbg_dump(YH[:, :], 16384, YPALL)
            S.recycle("YA")
            S.recycle("TM")

        if stage >= 10:
            def comb_p(dc, ts, bo, gt, gk, p):
                tp = tmf(2 + p)
                ktp = ("TM", "tp", p)

                def f(e):
                    return e.tensor_tensor(out=tp, in0=bank(bo), in1=gt, op=ALU.mult)
                S.op("dve", f, reads=[("PS", bo), gk], writes=[ktp])

                def f(e):
                    return e.tensor_tensor(out=mg[:, dc, ts], in0=tp, in1=m1[:, dc, ts], op=ALU.add)
                S.op("dve", f, reads=[ktp, ("KA", "m1", dc)], writes=[("YA", "mg", dc)])
            branch_out(wpo_v, 7168, yp, YPALL, comb_p)
            MGALL = [("YA", "mg", dc) for dc in range(8)]
            if stage == 10:
                dbg_dump(YA[:, :], 16384, MGALL)
            S.recycle("TM")
            S.recycle("UT")

            wo = []
            for half in range(2):
                wo.append(load_slab(wout_v[:, :, half * 512:(half + 1) * 512]))
            gfb = UT[:, 0:2048].bitcast(F32)
            S.dma("sp", [(gfb, gf_d.partition_broadcast(128)[:, 0, :])], writes=[("UT", "gfb")])
            S.recycle("KA")
            S.recycle("YH")
            junk5 = UT[:, 4096:5120]

            def p5_bufs(tc):
                q = tc % 4
                xt = KA[:, q * 2048:(q + 1) * 2048].bitcast(F32)
                ot = YH[:, q * 2048:(q + 1) * 2048].bitcast(F32)
                return xt, ot, ("KA", "xt", q), ("YH", "ot", q)

            def p5_xload(tc):
                xt, ot, kxt, kot = p5_bufs(tc)
                S.dma("sp", [(xt, x_d[tc * 128:(tc + 1) * 128, :])], writes=[kxt])

            for tc in range(4):
                p5_xload(tc)

            def p5_a(tc):
                p = tc % 2
                xt, ot, kxt, kot = p5_bufs(tc)
                for half in range(2):
                    b = 2 * p + half
                    S.op("pe", mm_group(bank(b), [(mg[:, j, tc * 128:(tc + 1) * 128], wo[half][0][:, j, :])
                                                  for j in range(8)]),
                         reads=MGALL + [wo[half][1]], writes=[("PS", b)])
                po = ps[:, 2 * p * 512:(2 * p + 2) * 512]

                def f(e):
                    return e.tensor_tensor(out=ot, in0=po, in1=xt, op=ALU.add)
                S.op("dve", f, reads=[("PS", 2 * p), ("PS", 2 * p + 1), kxt], writes=[kot])
                if tc + 4 < 16:
                    p5_xload(tc + 4)

                def f(e):
                    return e.activation(out=junk5, in_=ot, func=AF.Square, accum_out=ss2_sb[:, tc:tc + 1])
                S.op("act", f, reads=[kot], writes=[("UT", "junk5"), ("SM", "ss2", tc)])

                def f(e):
                    return e.activation(out=rs2_sb[:, tc:tc + 1], in_=ss2_sb[:, tc:tc + 1], func=AF.Sqrt,
                                        bias=epsc[:, 0:1], scale=1.0 / D)
                S.op("act", f, reads=[("SM", "ss2", tc), ("SM", "epsc")], writes=[("SM", "rs2", tc)])

            def p5_b(tc):
                xt, ot, kxt, kot = p5_bufs(tc)

                def f(e):
                    return e.reciprocal(out=rs2_sb[:, tc:tc + 1], in_=rs2_sb[:, tc:tc + 1])
                S.op("dve", f, reads=[("SM", "rs2", tc)], writes=[("SM", "rs2", tc)])

                def f(e):
                    return e.scalar_tensor_tensor(out=ot, in0=ot, scalar=rs2_sb[:, tc:tc + 1], in1=gfb,
                                                  op0=ALU.mult, op1=ALU.mult)
                S.op("dve", f, reads=[kot, ("SM", "rs2", tc), ("UT", "gfb")], writes=[kot])
                S.dma("sp", [(out_d[tc * 128:(tc + 1) * 128, :], ot)], reads=[kot], final=True)

            p5_a(0)
            for tc in range(16):
                if tc + 1 < 16:
                    p5_a(tc + 1)
                p5_b(tc)

        S.emit(st)
    return nc


def prep_inputs(inp):
    c = host_constants()
    f32 = lambda a: np.ascontiguousarray(np.asarray(a), dtype=np.float32)
    shared = {
        "g_norm": f32(np.asarray(inp["g_norm"])[0].reshape(8, 128).T),
        "w_in": f32(np.asarray(inp["w_in"])[0]),
        "b_in": f32(np.asarray(inp["b_in"])[0].reshape(64, 128).T),
        "conv_w": f32(np.asarray(inp["conv_w"])[0].reshape(3, 24, 128).transpose(2, 0, 1).reshape(128, 72)),
        "conv_b": f32(np.asarray(inp["conv_b"])[0].reshape(24, 128).T),
        "filt_w1": f32(np.asarray(inp["filt_w1"])[0]),
        "filt_w2": f32(np.asarray(inp["filt_w2"])[0]),
        "filt_w3": f32(np.asarray(inp["filt_w3"])[0]),
        "filt_b": f32(np.stack([np.asarray(inp["filt_b1"])[0], np.asarray(inp["filt_b2"])[0],
                                np.asarray(inp["filt_b3"])[0], np.asarray(inp["filt_freq"])[0]], axis=1)),
        "filt_w4": f32(np.asarray(inp["filt_w4"])[0]),
        "hyena_d": f32(np.asarray(inp["hyena_d"])[0].reshape(1, D)),
        "w_hyena_out": f32(np.asarray(inp["w_hyena_out"])[0]),
        "pool_w": f32(np.asarray(inp["pool_w"])[0]),
        "pool_bs": f32(np.concatenate([np.asarray(inp["pool_b"])[0].reshape(8, 128).T,
                                       np.asarray(inp["pool_scale"])[0].reshape(8, 128).T], axis=1)),
        "w_pool_out": f32(np.asarray(inp["w_pool_out"])[0]),
        "w_out": f32(np.asarray(inp["w_out"])[0]),
        "g_final": f32(np.asarray(inp["g_final"]).reshape(1, D)),
    }
    for k in ("cef", "cof", "sef", "sof", "cei", "sei", "coi", "soi", "ident", "zT", "negt", "deltas", "band"):
        shared[k] = c[k]
    x = np.asarray(inp["x"])
    in_maps = []
    for b in range(8):
        m = dict(shared)
        m["x"] = f32(x[b])
        in_maps.append(m)
    return in_maps


_PROG = {}


def kernel(**inputs):
    in_maps = prep_inputs(inputs)
    if "nc" not in _PROG:
        _PROG["nc"] = build_program()
    res = run_bass_kernel_spmd(_PROG["nc"], in_maps, core_ids=list(range(8)))
    out = np.stack([np.asarray(r["out"], dtype=np.float32) for r in res.results], axis=0)
    return out
```
